# Optimizing a Trainium2 kernel written in Bass

```python
import math
import jax, jax.numpy as jnp
from jax import lax
import numpy as np

D_MODEL = 1024
BATCH = 1
SEQ = 16384
DEPTH = 2

GRID_W = 64
CTX_LEN = 256
N_Q_HEADS = 8
N_KV_HEADS = 2
GQA_GROUP = N_Q_HEADS // N_KV_HEADS
HEAD_DIM = 128
ATTN_WIDTH = N_Q_HEADS * HEAD_DIM
KV_WIDTH = N_KV_HEADS * HEAD_DIM
ROPE_THETA = 10000.0
ROPE_FREQS = HEAD_DIM // 4
Q_BLOCK = 128
ATTN_SCALE = 1.0 / math.sqrt(HEAD_DIM)
GMLP_WIDTH = D_MODEL
GMLP_GROUPS = 4
GMLP_GROUP_DIM = GMLP_WIDTH // GMLP_GROUPS
CHUNK = 128
CONV_WIDTH = D_MODEL
CONV_KERNEL = 31
CONV_PAD = CONV_KERNEL // 2
N_BRANCHES = 3
D_FF = 4 * D_MODEL
EPS = 1e-6
Q_OFF = 0
K_OFF = Q_OFF + ATTN_WIDTH
V_OFF = K_OFF + KV_WIDTH
GU_OFF = V_OFF + KV_WIDTH
GV_OFF = GU_OFF + GMLP_WIDTH
CG_OFF = GV_OFF + GMLP_WIDTH
GATE_OFF = CG_OFF + 2 * CONV_WIDTH
IN_COLS = GATE_OFF + N_BRANCHES * D_MODEL

kernel_name = 'hybrid_gqa_gmlp_conformer_dit_block'


def rms_norm(x, g):
    xf = x.astype(jnp.float32)
    y = xf * lax.rsqrt(jnp.mean(xf * xf, axis=-1, keepdims=True) + EPS)
    return (y * g.astype(jnp.float32)).astype(x.dtype)


def layer_norm(x, g, b):
    xf = x.astype(jnp.float32)
    mu = jnp.mean(xf, axis=-1, keepdims=True)
    var = jnp.mean(jnp.square(xf - mu), axis=-1, keepdims=True)
    y = (xf - mu) * lax.rsqrt(var + EPS)
    return (y * g.astype(jnp.float32) + b.astype(jnp.float32)).astype(x.dtype)


def modulate(h, shift, scale):
    return h * (1.0 + scale) + shift


def axial_rope_tables(n_rows):
    row = jnp.repeat(jnp.arange(n_rows), GRID_W).astype(jnp.float32)
    col = jnp.tile(jnp.arange(GRID_W), n_rows).astype(jnp.float32)
    inv = ROPE_THETA ** (-jnp.arange(ROPE_FREQS, dtype=jnp.float32) / ROPE_FREQS)
    ang = jnp.concatenate([row[:, None] * inv, col[:, None] * inv], axis=-1)
    return jnp.cos(ang), jnp.sin(ang)


def apply_rope(x, cos, sin):
    half = HEAD_DIM // 2
    x1, x2 = x[..., :half], x[..., half:]
    c = cos[None, :, None, :].astype(x.dtype)
    s = sin[None, :, None, :].astype(x.dtype)
    return jnp.concatenate([x1 * c - x2 * s, x2 * c + x1 * s], axis=-1)


def q_heads(p, q_norm_g):
    B, L, _ = p.shape
    return rms_norm(p[..., Q_OFF:K_OFF].reshape(B, L, N_Q_HEADS, HEAD_DIM), q_norm_g)


def kv_heads(pkv, k_norm_g):
    B, L, _ = pkv.shape
    k = rms_norm(pkv[..., :KV_WIDTH].reshape(B, L, N_KV_HEADS, HEAD_DIM), k_norm_g)
    v = pkv[..., KV_WIDTH:].reshape(B, L, N_KV_HEADS, HEAD_DIM)
    return k, v


def gqa_attend(q, k, v):
    s = jnp.einsum('bqhgd,bkhd->bhgqk', q, k).astype(jnp.float32) * ATTN_SCALE
    p = jax.nn.softmax(s, axis=-1).astype(v.dtype)
    return jnp.einsum('bhgqk,bkhd->bqhgd', p, v)


def latent_attention(q, k_all, v_all):
    B, S = q.shape[0], q.shape[1]
    nb = S // Q_BLOCK
    qb = q.reshape(B, nb, Q_BLOCK, N_KV_HEADS, GQA_GROUP, HEAD_DIM).transpose(1, 0, 2, 3, 4, 5)
    ob = lax.map(lambda qi: gqa_attend(qi, k_all, v_all), qb)
    return ob.transpose(1, 0, 2, 3, 4, 5).reshape(B, S, ATTN_WIDTH)


def chunk_spatial_gate(u, v, ws, bs):
    B, L, _ = v.shape
    nc = L // CHUNK
    vg = v.reshape(B, nc, CHUNK, GMLP_GROUPS, GMLP_GROUP_DIM)
    sv = jnp.einsum('gpq,bnqgc->bnpgc', ws, vg) + bs.T[None, None, :, :, None]
    return u * sv.reshape(B, L, GMLP_WIDTH)


def conformer_conv(a, conv_w, conv_b, norm_g, norm_b, w_o):
    gl = a[..., :CONV_WIDTH] * jax.nn.sigmoid(a[..., CONV_WIDTH:])
    y = lax.conv_general_dilated(gl, conv_w[:, None, :], window_strides=(1,),
                                 padding=[(CONV_PAD, CONV_PAD)],
                                 dimension_numbers=('NWC', 'WIO', 'NWC'),
                                 feature_group_count=CONV_WIDTH) + conv_b
    y = jax.nn.silu(layer_norm(y, norm_g, norm_b))
    return y @ w_o


def mixer_merge(p, attn, w_attn_o, gmlp_norm_g, gmlp_ws, gmlp_bs, w_gmlp_o,
                conv_w, conv_b, conv_norm_g, conv_norm_b, w_conv_o, w_out):
    a_out = attn @ w_attn_o
    gu = jax.nn.gelu(p[..., GU_OFF:GV_OFF])
    gv = rms_norm(jax.nn.gelu(p[..., GV_OFF:CG_OFF]), gmlp_norm_g)
    g_out = chunk_spatial_gate(gu, gv, gmlp_ws, gmlp_bs) @ w_gmlp_o
    c_out = conformer_conv(p[..., CG_OFF:GATE_OFF], conv_w, conv_b, conv_norm_g, conv_norm_b, w_conv_o)
    gates = jax.nn.sigmoid(p[..., GATE_OFF:])
    ga = gates[..., :D_MODEL]
    gg = gates[..., D_MODEL:2 * D_MODEL]
    gc = gates[..., 2 * D_MODEL:]
    return (ga * a_out + gg * g_out + gc * c_out) @ w_out


def sq_relu_mlp(h, w1, w2):
    return jnp.square(jax.nn.relu(h @ w1)) @ w2


def setup_inputs(seed: int = 0) -> dict:
    key = jax.random.key(seed)
    ks = jax.random.split(key, 32)
    f32 = jnp.float32

    def nrm(k, shape, scale):
        return jax.random.normal(k, shape, f32) * scale

    def gain(k, shape):
        return 1.0 + 0.1 * jax.random.normal(k, shape, f32)

    L = DEPTH
    return {
        'x': nrm(ks[0], (BATCH, SEQ, D_MODEL), 1.0),
        'c': nrm(ks[1], (BATCH, D_MODEL), 1.0),
        'ctx': nrm(ks[2], (BATCH, CTX_LEN, D_MODEL), 1.0),
        'c_ctx': nrm(ks[3], (D_MODEL,), 1.0),
        'ada_w': nrm(ks[4], (L, D_MODEL, 6 * D_MODEL), 0.5 * D_MODEL ** -0.5),
        'ada_b': nrm(ks[5], (L, 6 * D_MODEL), 0.02),
        'mix_pre_g': gain(ks[6], (L, D_MODEL)),
        'mix_post_g': gain(ks[7], (L, D_MODEL)),
        'w_in': nrm(ks[8], (L, D_MODEL, IN_COLS), D_MODEL ** -0.5),
        'q_norm_g': gain(ks[9], (L, HEAD_DIM)),
        'k_norm_g': gain(ks[10], (L, HEAD_DIM)),
        'w_attn_o': nrm(ks[11], (L, ATTN_WIDTH, D_MODEL), ATTN_WIDTH ** -0.5),
        'gmlp_norm_g': gain(ks[12], (L, GMLP_WIDTH)),
        'gmlp_ws': nrm(ks[13], (L, GMLP_GROUPS, CHUNK, CHUNK), CHUNK ** -0.5),
        'gmlp_bs': gain(ks[14], (L, GMLP_GROUPS, CHUNK)),
        'w_gmlp_o': nrm(ks[15], (L, GMLP_WIDTH, D_MODEL), GMLP_WIDTH ** -0.5),
        'conv_w': nrm(ks[16], (L, CONV_KERNEL, CONV_WIDTH), CONV_KERNEL ** -0.5),
        'conv_b': nrm(ks[17], (L, CONV_WIDTH), 0.02),
        'conv_norm_g': gain(ks[18], (L, CONV_WIDTH)),
        'conv_norm_b': nrm(ks[19], (L, CONV_WIDTH), 0.02),
        'w_conv_o': nrm(ks[20], (L, CONV_WIDTH, D_MODEL), CONV_WIDTH ** -0.5),
        'w_out': nrm(ks[21], (L, D_MODEL, D_MODEL), D_MODEL ** -0.5),
        'ffn_pre_g': gain(ks[22], (L, D_MODEL)),
        'ffn_post_g': gain(ks[23], (L, D_MODEL)),
        'w_ff1': nrm(ks[24], (L, D_MODEL, D_FF), D_MODEL ** -0.5),
        'w_ff2': nrm(ks[25], (L, D_FF, D_MODEL), D_FF ** -0.5),
    }


def reference(x, c, ctx, c_ctx, ada_w, ada_b, mix_pre_g, mix_post_g, w_in, q_norm_g, k_norm_g,
              w_attn_o, gmlp_norm_g, gmlp_ws, gmlp_bs, w_gmlp_o, conv_w, conv_b, conv_norm_g,
              conv_norm_b, w_conv_o, w_out, ffn_pre_g, ffn_post_g, w_ff1, w_ff2):
    B, S, _ = x.shape
    n_ctx = ctx.shape[1]
    ROWS = S // GRID_W
    cos, sin = axial_rope_tables(ROWS)
    xc = ctx
    for l in range(DEPTH):
        last = l == DEPTH - 1
        mod = (jax.nn.silu(c) @ ada_w[l] + ada_b[l])[:, None, :]
        sh1, sc1, g1, sh2, sc2, g2 = jnp.split(mod, 6, axis=-1)
        modc = jax.nn.silu(c_ctx) @ ada_w[l] + ada_b[l]
        shc1, scc1, gc1, shc2, scc2, gc2 = jnp.split(modc, 6, axis=-1)

        h = modulate(rms_norm(x, mix_pre_g[l]), sh1, sc1)
        hc = modulate(rms_norm(xc, mix_pre_g[l]), shc1, scc1)
        p = h @ w_in[l]
        q = apply_rope(q_heads(p, q_norm_g[l]), cos, sin)
        k, v = kv_heads(p[..., K_OFF:GU_OFF], k_norm_g[l])
        k = apply_rope(k, cos, sin)
        if last:
            kc, vc = kv_heads(hc @ w_in[l][:, K_OFF:GU_OFF], k_norm_g[l])
        else:
            pc = hc @ w_in[l]
            qc = q_heads(pc, q_norm_g[l])
            kc, vc = kv_heads(pc[..., K_OFF:GU_OFF], k_norm_g[l])
        k_all = jnp.concatenate([kc, k], axis=1)
        v_all = jnp.concatenate([vc, v], axis=1)
        attn = latent_attention(q.reshape(B, S, N_KV_HEADS, GQA_GROUP, HEAD_DIM), k_all, v_all)
        y = mixer_merge(p, attn, w_attn_o[l], gmlp_norm_g[l], gmlp_ws[l], gmlp_bs[l], w_gmlp_o[l],
                        conv_w[l], conv_b[l], conv_norm_g[l], conv_norm_b[l], w_conv_o[l], w_out[l])
        x = x + g1 * rms_norm(y, mix_post_g[l])

        h2 = modulate(rms_norm(x, ffn_pre_g[l]), sh2, sc2)
        x = x + g2 * rms_norm(sq_relu_mlp(h2, w_ff1[l], w_ff2[l]), ffn_post_g[l])

        if not last:
            attn_c = gqa_attend(qc.reshape(B, n_ctx, N_KV_HEADS, GQA_GROUP, HEAD_DIM), kc, vc)
            yc = mixer_merge(pc, attn_c.reshape(B, n_ctx, ATTN_WIDTH), w_attn_o[l], gmlp_norm_g[l],
                             gmlp_ws[l], gmlp_bs[l], w_gmlp_o[l], conv_w[l], conv_b[l],
                             conv_norm_g[l], conv_norm_b[l], w_conv_o[l], w_out[l])
            xc = xc + gc1 * rms_norm(yc, mix_post_g[l])
            hc2 = modulate(rms_norm(xc, ffn_pre_g[l]), shc2, scc2)
            xc = xc + gc2 * rms_norm(sq_relu_mlp(hc2, w_ff1[l], w_ff2[l]), ffn_post_g[l])
    return x
```

```python
import math
import contextlib
import numpy as np
import concourse.bass as bass
import concourse.mybir as mybir
from concourse.bass_utils import run_bass_kernel_spmd

F32 = mybir.dt.float32
BF16 = mybir.dt.bfloat16
AF = mybir.ActivationFunctionType
ALU = mybir.AluOpType

NCORES = 8
D = 1024
SEQ = 16384
TOK = SEQ // NCORES
CTX = 256
DEPTH = 2
GRID_W = 64
HD = 128
NQH = 8
NKVH = 2
KC = D // 128
EPS = 1e-6
ATTN_SCALE = 1.0 / math.sqrt(HD)
Q_OFF, K_OFF, V_OFF, GU_OFF, GV_OFF, CG_OFF = 0, 1024, 1280, 1536, 2560, 3584
GATE_OFF = 5632
IN_COLS = 8704
DFF = 4096
CK = 31
TT = 512
NKEY = CTX + SEQ
NKB = NKEY // 128
HALO = 16
Q_SUB = 99
STOP_STAGE = 99


class Reg:
    __slots__ = ("name", "w", "r", "excl")

    def __init__(self, name):
        self.name = name
        self.w = {}
        self.r = {}
        self.excl = False


class Ten:
    def __init__(self, h, name):
        self.h = h
        self.reg = Reg(name)

    def __getitem__(self, k):
        return self.h[k]


class Eng:
    def __init__(self, name, h, sem):
        self.name = name
        self.h = h
        self.sem = sem
        self.cnt = 0
        self.waited = {}


class KB:
    def __init__(self, nc, es):
        self.nc = nc
        self.es = es
        self.sems = {}
        mk = lambda n: es.enter_context(nc.semaphore(n))
        self.pe = Eng("pe", nc.tensor, mk("s_pe"))
        self.act = Eng("act", nc.scalar, mk("s_act"))
        self.dve = Eng("dve", nc.vector, mk("s_dve"))
        self.pool = Eng("pool", nc.gpsimd, mk("s_pool"))
        self.sp = Eng("sp", nc.sync, None)
        self.engs = [self.pe, self.act, self.dve, self.pool, self.sp]
        for e in self.engs:
            if e.sem is not None:
                self.sems[id(e.sem)] = e.sem
        self.dsem = {}
        for q, n in (("sp", 28), ("pool", 12)):
            lst = []
            for i in range(n):
                s = mk(f"d_{q}{i}")
                self.sems[id(s)] = s
                lst.append([s, 0])
            self.dsem[q] = [lst, 0]
        self.ccsem = [mk("s_cc"), 0]
        self.sems[id(self.ccsem[0])] = self.ccsem[0]
        self.n_inst = 0

    def _deps(self, eng, reads, writes):
        deps = {}
        for R in reads:
            for k, v in R.w.items():
                if deps.get(k, 0) < v:
                    deps[k] = v
            if R.excl:
                for k, v in R.r.items():
                    if deps.get(k, 0) < v:
                        deps[k] = v
        for R in writes:
            for dct in (R.w, R.r):
                for k, v in dct.items():
                    if deps.get(k, 0) < v:
                        deps[k] = v
        for k, v in deps.items():
            if eng is self.pe and k == id(self.pe.sem):
                continue
            if eng.waited.get(k, 0) < v:
                eng.h.wait_ge(self.sems[k], v)
                eng.waited[k] = v
                self.n_inst += 1

    def _mark(self, reads, writes, key, val):
        for R in reads:
            if R.r.get(key, 0) < val:
                R.r[key] = val
        for R in writes:
            if R.w.get(key, 0) < val:
                R.w[key] = val

    @staticmethod
    def _regs(lst):
        return [x.reg if isinstance(x, Ten) else x for x in lst]

    def op(self, eng, fn, reads, writes, inc=True):
        reads = self._regs(reads)
        writes = self._regs(writes)
        self._deps(eng, reads, writes)
        ins = fn()
        self.n_inst += 1
        if inc:
            eng.cnt += 1
            ins.then_inc(eng.sem, 1)
            val = eng.cnt
        else:
            val = eng.cnt + 1
        self._mark(reads, writes, id(eng.sem), val)
        return ins

    def dma(self, q, out, in_, reads, writes):
        eng = self.sp if q == "sp" else self.pool
        reads = self._regs(reads)
        writes = self._regs(writes)
        lst, idx = self.dsem[q]
        ent = lst[idx]
        self.dsem[q][1] = (idx + 1) % len(lst)
        sem, val = ent
        if val > 0 and eng.waited.get(id(sem), 0) < val:
            eng.h.wait_ge(sem, val)
            eng.waited[id(sem)] = val
        self._deps(eng, reads, writes)
        eng.h.dma_start(out=out, in_=in_).then_inc(sem, 16)
        self.n_inst += 1
        ent[1] = val + 16
        self._mark(reads, writes, id(sem), val + 16)

    def allgather(self, in_ap, out_ap, reads, writes):
        eng = self.pool
        reads = self._regs(reads)
        writes = self._regs(writes)
        self._deps(eng, reads, writes)
        sem = self.ccsem[0]
        eng.h.collective_compute("AllGather", ALU.bypass, replica_groups=[list(range(NCORES))],
                                 ins=[in_ap.opt()], outs=[out_ap.opt()]).then_inc(sem)
        self.ccsem[1] += 1
        self._mark(reads, writes, id(sem), self.ccsem[1])

    def barrier(self):
        targets = {}
        for e in (self.pe, self.act, self.dve, self.pool):
            if e.cnt > 0:
                targets[id(e.sem)] = e.cnt
        for q in ("sp", "pool"):
            for s, v in self.dsem[q][0]:
                if v > 0:
                    targets[id(s)] = v
        if self.ccsem[1] > 0:
            targets[id(self.ccsem[0])] = self.ccsem[1]
        for e in self.engs:
            for k, v in targets.items():
                if e is self.pe and k == id(self.pe.sem):
                    continue
                if e.waited.get(k, 0) < v:
                    e.h.wait_ge(self.sems[k], v)
                    e.waited[k] = v

    def sb(self, stack, name, shape, dt):
        self.uid = getattr(self, "uid", 0) + 1
        name = f"sb{self.uid}_{name}"
        return Ten(stack.enter_context(self.nc.sbuf_tensor(name, shape, dt)), name)

    def dram(self, name, shape, dt, kind="Internal"):
        return Ten(self.nc.dram_tensor(name, shape, dt, kind=kind), name)


def build_program(debug=False, n_layers=DEPTH):
    nc = bass.Bass("TRN2", target_bir_lowering=False)
    es = contextlib.ExitStack()
    with es:
        _emit(nc, es, debug, n_layers)
    return nc


def _emit(nc, es, debug, n_layers):
    kb = KB(nc, es)
    PE, ACT, DVE, POOL = kb.pe, kb.act, kb.dve, kb.pool
    L = DEPTH

    def ext_in(name, shape, dt=F32):
        return kb.dram(name, shape, dt, kind="ExternalInput")

    xT_in = ext_in("xT", [D, TOK])
    ctxT_in = ext_in("ctxT", [D, CTX])
    cT_in = ext_in("cT", [128, KC * 2])
    ada_w = ext_in("ada_w", [L, D, 6 * D])
    w_in = ext_in("w_in", [L, D, IN_COLS])
    w_attn_o = ext_in("w_attn_o", [L, D, D])
    w_gmlp_o = ext_in("w_gmlp_o", [L, D, D])
    w_conv_o = ext_in("w_conv_o", [L, D, D])
    w_out = ext_in("w_out", [L, D, D])
    w_ff1 = ext_in("w_ff1", [L, D, DFF])
    w_ff2 = ext_in("w_ff2", [L, DFF, D])
    NV = 7
    vecs_in = ext_in("vecs", [L, 128, NV * KC])
    adab_in = ext_in("adab", [L, 128, 48])
    qkg_in = ext_in("qkg", [L, 128, 2])
    gng_in = ext_in("gng", [L, 128, D])
    convw_in = ext_in("convw", [L, 128, KC * CK])
    wsT_in = ext_in("wsT", [L, 128, 4 * 128])
    bsrow_in = ext_in("bsrow", [L, 1, 512])
    cos_in = ext_in("cosT", [128, TOK])
    sin_in = ext_in("sinT", [128, TOK])
    ident_in = ext_in("ident", [128, 128])
    perm_in = ext_in("perm", [128, 128])
    sel_in = ext_in("sel", [128, 16])
    out_kind = "ExternalOutput"
    outT = kb.dram("outT", [D, TOK], F32, kind=out_kind)

    dbg_kind = "ExternalOutput" if debug else "Internal"
    NT = TOK + CTX
    xres = kb.dram("xres", [D, TOK], F32, kind=dbg_kind)
    xcres = kb.dram("xcres", [D, CTX], F32, kind=dbg_kind)
    hs = kb.dram("hs", [D, NT], BF16, kind=dbg_kind)
    qs = kb.dram("qs", [D, NT], BF16, kind=dbg_kind)
    attn_s = kb.dram("attn_s", [D, NT], BF16, kind=dbg_kind)
    kloc = kb.dram("kloc", [NKVH * 128, TOK], BF16)
    vloc = kb.dram("vloc", [NKVH * 128, TOK], BF16)
    kall = kb.dram("kall", [NCORES * NKVH * 128, TOK], BF16)
    vall = kb.dram("vall", [NCORES * NKVH * 128, TOK], BF16)
    kcloc = kb.dram("kcloc", [NKVH * 128, CTX], BF16, kind=dbg_kind)
    vcloc = kb.dram("vcloc", [NKVH * 128, CTX], BF16, kind=dbg_kind)
    GLW = HALO + TOK + HALO
    gls = kb.dram("gls", [D, GLW], BF16, kind=dbg_kind)
    GCW = HALO + CTX + HALO
    glc = kb.dram("glc", [D, GCW], BF16)
    eloc = kb.dram("eloc", [128, 2 * KC * HALO], BF16)
    eall = kb.dram("eall", [NCORES * 128, 2 * KC * HALO], BF16)

    dbgc = kb.dram("dbgc", [5 * D, TT], F32, kind=dbg_kind)

    def fm(t, c0, c1):
        return t.h.ap().rearrange("(k p) c -> p k c", p=128)[:, :, c0:c1]

    P = es
    ones_bf = kb.sb(P, "ones_bf", [128, 128], BF16)
    ident_bf = kb.sb(P, "ident_bf", [128, 128], BF16)
    perm_bf = kb.sb(P, "perm_bf", [128, 128], BF16)
    onesrow = kb.sb(P, "onesrow", [1, 128], BF16)
    sel = kb.sb(P, "sel", [128, 16], F32)
    sc = kb.sb(P, "sc", [128, KC * 2], F32)
    ebuf = kb.sb(P, "ebuf", [128, 2, KC, HALO], BF16)
    halb = kb.sb(P, "halb", [128, 2, KC, HALO], BF16)
    WB = [kb.sb(P, f"wb{i}", [128, 8192], BF16) for i in range(3)]
    NTF, NTB = 6, 5
    TF = [kb.sb(P, f"tf{i}", [128, TT], F32) for i in range(NTF)]
    TB = [kb.sb(P, f"tb{i}", [128, TT], BF16) for i in range(NTB)]
    RS = [kb.sb(P, f"rs{i}", [128, TT], F32) for i in range(2)]
    rsi = [0]
    tfi = [0]
    tbi = [0]

    def tf():
        t = TF[tfi[0] % NTF]
        tfi[0] += 1
        return t

    def tb():
        t = TB[tbi[0] % NTB]
        tbi[0] += 1
        return t

    PS = [Ten(es.enter_context(nc.psum_tensor(f"ps{i}", [128, 512], F32)), f"ps{i}") for i in range(8)]
    for _p in PS:
        _p.reg.excl = True
    ps_free = list(range(8))

    def ps_alloc():
        assert ps_free, "out of PSUM banks"
        return PS[ps_free.pop(0)]

    def ps_release(t):
        ps_free.append(PS.index(t))

    vecs = kb.sb(P, "vecs", [128, NV, KC], F32)
    adab = kb.sb(P, "adab", [128, 48], F32)
    qkg = kb.sb(P, "qkg", [128, 2], F32)
    gng = kb.sb(P, "gng", [128, D], F32)
    convw = kb.sb(P, "convw", [128, KC, CK], F32)
    wsT = kb.sb(P, "wsT", [128, 4, 128], BF16)
    bsrow = kb.sb(P, "bsrow", [1, 512], BF16)
    modv = kb.sb(P, "modv", [128, 48, 2], F32)
    der = kb.sb(P, "der", [128, 2, 6, KC], F32)

    def act(out, in_, func, reads, writes, bias=None, scale=None, accum_out=None):
        kw = {}
        if bias is not None:
            kw["bias"] = bias
        if scale is not None:
            kw["scale"] = scale
        if accum_out is not None:
            kw["accum_out"] = accum_out
        return kb.op(ACT, lambda: nc.scalar.activation(out=out, in_=in_, func=func, **kw), reads, writes)

    def tt(out, in0, in1, op, reads, writes, eng=None):
        e = eng or DVE
        return kb.op(e, lambda: e.h.tensor_tensor(out=out, in0=in0, in1=in1, op=op), reads, writes)

    def ts(out, in0, s1, s2, op0, op1, reads, writes, eng=None):
        e = eng or DVE
        if s2 is None:
            return kb.op(e, lambda: e.h.tensor_scalar(out=out, in0=in0, scalar1=s1, scalar2=None, op0=op0), reads, writes)
        return kb.op(e, lambda: e.h.tensor_scalar(out=out, in0=in0, scalar1=s1, scalar2=s2, op0=op0, op1=op1), reads, writes)

    def stt(out, in0, scalar, in1, op0, op1, reads, writes):
        return kb.op(DVE, lambda: nc.vector.scalar_tensor_tensor(out=out, in0=in0, scalar=scalar, in1=in1, op0=op0, op1=op1), reads, writes)

    def recip(out, in_, reads, writes):
        return kb.op(DVE, lambda: nc.vector.reciprocal(out=out, in_=in_), reads, writes)

    def copy(eng, out, in_, reads, writes):
        if eng is ACT:
            return act(out, in_, AF.Copy, reads, writes)
        return kb.op(eng, lambda: eng.h.tensor_copy(out=out, in_=in_), reads, writes)

    def memset(eng, ap, val, writes):
        return kb.op(eng, lambda: eng.h.memset(ap, val), [], writes)

    def mm(out, lhsT, rhs, start, stop, reads, writes, inc=None):
        return kb.op(PE, lambda: nc.tensor.matmul(out, lhsT, rhs, start=start, stop=stop), reads, writes,
                     inc=(stop if inc is None else inc))

    memset(DVE, ones_bf[:], 1.0, [ones_bf])
    memset(DVE, onesrow[:], 1.0, [onesrow])
    kb.dma("pool", ident_bf[:], ident_in.h.ap(), [ident_in], [ident_bf])
    kb.dma("pool", perm_bf[:], perm_in.h.ap(), [perm_in], [perm_bf])
    kb.dma("sp", sel[:], sel_in.h.ap(), [sel_in], [sel])
    ctmp = kb.sb(P, "ctmp", [128, KC * 2], F32)
    kb.dma("sp", ctmp[:], cT_in.h.ap(), [cT_in], [ctmp])
    act(sc[:], ctmp[:], AF.Silu, [ctmp], [sc])

    class WStream:
        def __init__(self):
            self.specs = []
            self.issued = 0
            self.slot = 0

        def add(self, src_ten, src_ap, kc, cols, cast=True):
            self.specs.append((src_ten, src_ap, kc, cols, cast))
            return len(self.specs) - 1

        def _issue(self, j):
            src_ten, src_ap, kc, cols, cast = self.specs[j]
            wbuf = WB[j % 3]
            if cast:
                dst = wbuf[:, 0:kc * cols].rearrange("p (k c) -> p k c", k=kc)
                kb.dma("pool", dst, src_ap, [src_ten], [wbuf])
            else:
                dst = wbuf[:].bitcast(F32)[:, 0:kc * cols].rearrange("p (k c) -> p k c", k=kc)
                kb.dma("sp", dst, src_ap, [src_ten], [wbuf])

        def get(self, j):
            while self.issued < min(j + 2, len(self.specs)):
                self._issue(self.issued)
                self.issued += 1
            src_ten, src_ap, kc, cols, cast = self.specs[j]
            wbuf = WB[j % 3]
            if cast:
                view = wbuf[:, 0:kc * cols].rearrange("p (k c) -> p k c", k=kc)
            else:
                view = wbuf[:].bitcast(F32)[:, 0:kc * cols].rearrange("p (k c) -> p k c", k=kc)
            return wbuf, view

    def wsrc(t, l, r0, r1, c0, c1):
        return t.h.ap()[l, r0:r1, c0:c1].rearrange("(k p) c -> p k c", p=128)

    def rms_stats(src_ap_fn, src_regs, nch, T, denom):
        ss = ps_alloc()
        for kc in range(nch):
            sq = tb()
            act(sq[:, :T], src_ap_fn(kc), AF.Square, src_regs, [sq])
            mm(ss[:, :T], ones_bf[:], sq[:, :T], kc == 0, kc == nch - 1, [ones_bf, sq], [ss], inc=True)
        std = tf()
        act(std[:, :T], ss[:, :T], AF.Sqrt, [ss], [std], bias=EPS, scale=1.0 / denom)
        ps_release(ss)
        rstd = RS[rsi[0] % 2]
        rsi[0] += 1
        recip(rstd[:, :T], std[:, :T], [std], [rstd])
        return rstd

    def normmod(xt, T, A, B, hT):
        rstd = rms_stats(lambda kc: xt[:, kc, :T], [xt], KC, T, float(D))
        for kc in range(KC):
            tmp = tf()
            tt(tmp[:, :T], xt[:, kc, :T], rstd[:, :T], ALU.mult, [xt, rstd], [tmp])
            act(hT[:, kc, :T], tmp[:, :T], AF.Identity, [tmp, der], [hT],
                bias=B[:, kc:kc + 1], scale=A[:, kc:kc + 1])

    def postnorm_res(Y, T, G, xt):
        rstd = rms_stats(lambda kc: Y[:, kc, :T], [Y], KC, T, float(D))
        for kc in range(KC):
            tmp = tf()
            tt(tmp[:, :T], Y[:, kc, :T], rstd[:, :T], ALU.mult, [Y, rstd], [tmp])
            stt(xt[:, kc, :T], tmp[:, :T], G[:, kc:kc + 1], xt[:, kc, :T], ALU.mult, ALU.add,
                [tmp, der, xt], [xt])

    def proj(psT, wbuf, wview, col0, src, T, nkc=KC):
        for kc in range(nkc):
            mm(psT[:, :T], wview[:, kc, col0:col0 + 128], src[:, kc, :T], kc == 0, kc == nkc - 1,
               [wbuf, src], [psT])

    for l in range(n_layers):
        last = (l == DEPTH - 1)
        x_src = xT_in if l == 0 else xres
        xc_src = ctxT_in if l == 0 else xcres
        x_dst = outT if last else xres

        kb.dma("sp", vecs[:].rearrange("p v k -> p (v k)"), vecs_in.h.ap()[l], [vecs_in], [vecs])
        kb.dma("sp", adab[:], adab_in.h.ap()[l], [adab_in], [adab])
        kb.dma("sp", qkg[:], qkg_in.h.ap()[l], [qkg_in], [qkg])
        kb.dma("sp", gng[:], gng_in.h.ap()[l], [gng_in], [gng])
        kb.dma("sp", convw[:].rearrange("p k c -> p (k c)"), convw_in.h.ap()[l], [convw_in], [convw])
        kb.dma("pool", wsT[:].rearrange("p g q -> p (g q)"), wsT_in.h.ap()[l], [wsT_in], [wsT])
        kb.dma("pool", bsrow[:], bsrow_in.h.ap()[l], [bsrow_in], [bsrow])

        wsm = WStream()
        for gi in range(12):
            wsm.add(ada_w, wsrc(ada_w, l, 0, D, gi * 512, (gi + 1) * 512), KC, 512, cast=False)
        for gi in range(12):
            wbuf, wv = wsm.get(gi)
            pm = ps_alloc()
            for j in range(4):
                for kc in range(KC):
                    mm(pm[:, j * 2:j * 2 + 2], wv[:, kc, j * 128:(j + 1) * 128], sc[:, kc * 2:kc * 2 + 2],
                       kc == 0, kc == KC - 1, [wbuf, sc], [pm])
            for j in range(4):
                cg = gi * 4 + j
                ts(modv[:, cg, :], pm[:, j * 2:j * 2 + 2], adab[:, cg:cg + 1], None, ALU.add, None,
                   [pm, adab], [modv])
            ps_release(pm)
        for w in range(2):
            stt(der[:, w, 0, :], modv[:, 8:16, w], 1.0, vecs[:, 0, :], ALU.add, ALU.mult, [modv, vecs], [der])
            copy(DVE, der[:, w, 1, :], modv[:, 0:8, w], [modv], [der])
            tt(der[:, w, 2, :], modv[:, 16:24, w], vecs[:, 1, :], ALU.mult, [modv, vecs], [der])
            stt(der[:, w, 3, :], modv[:, 32:40, w], 1.0, vecs[:, 5, :], ALU.add, ALU.mult, [modv, vecs], [der])
            copy(DVE, der[:, w, 4, :], modv[:, 24:32, w], [modv], [der])
            tt(der[:, w, 5, :], modv[:, 40:48, w], vecs[:, 6, :], ALU.mult, [modv, vecs], [der])
        kb.barrier()
        if STOP_STAGE <= 1:
            return

        tiles = [(False, i * TT, TT, i * TT) for i in range(TOK // TT)] + [(True, 0, CTX, TOK)]

        with contextlib.ExitStack() as ph:
            xt = kb.sb(ph, "a_xt", [128, KC, TT], F32)
            hT = kb.sb(ph, "a_hT", [128, KC, TT], BF16)
            glT = kb.sb(ph, "a_glT", [128, KC, TT], BF16)
            qT = kb.sb(ph, "a_qT", [128, NQH, TT], BF16)
            kT = kb.sb(ph, "a_kT", [128, NKVH, TT], BF16)
            vt = kb.sb(ph, "a_vt", [128, TT // 128, NKVH * 128], BF16)
            cosT = kb.sb(ph, "a_cos", [128, TT], F32)
            sinT = kb.sb(ph, "a_sin", [128, TT], F32)

            wsA = WStream()
            for (is_ctx, t0, T, co) in tiles:
                need_q = not (is_ctx and last)
                if need_q:
                    wsA.add(w_in, wsrc(w_in, l, 0, D, Q_OFF, Q_OFF + 1024), KC, 1024)
                wsA.add(w_in, wsrc(w_in, l, 0, D, K_OFF, K_OFF + 512), KC, 512)
                if not (is_ctx and last):
                    wsA.add(w_in, wsrc(w_in, l, 0, D, CG_OFF, CG_OFF + 1024), KC, 1024)
                    wsA.add(w_in, wsrc(w_in, l, 0, D, CG_OFF + 1024, CG_OFF + 2048), KC, 1024)
            wi = 0

            def qk_post(pq, T, gcol, rope, out_ap, out_reg):
                if Q_SUB <= 0:
                    return
                sq = tb()
                act(sq[:, :T], pq[:, :T], AF.Square, [pq], [sq])
                if Q_SUB <= 1:
                    return
                qg = tb()
                ts(qg[:, :T], pq[:, :T], qkg[:, gcol:gcol + 1], None, ALU.mult, None, [pq, qkg], [qg])
                if Q_SUB <= 2:
                    return
                ss = ps_alloc()
                mm(ss[:, :T], ones_bf[:], sq[:, :T], True, True, [ones_bf, sq], [ss])
                std = tf()
                act(std[:, :T], ss[:, :T], AF.Sqrt, [ss], [std], bias=EPS, scale=1.0 / HD)
                ps_release(ss)
                rstd = tf()
                recip(rstd[:, :T], std[:, :T], [std], [rstd])
                if Q_SUB <= 3:
                    return
                if rope:
                    psw = ps_alloc()
                    mm(psw[:, :T], perm_bf[:], qg[:, :T], True, True, [perm_bf, qg], [psw])
                    if Q_SUB <= 4:
                        ps_release(psw)
                        return
                    t1 = tf()
                    tt(t1[:, :T], qg[:, :T], cosT[:, :T], ALU.mult, [qg, cosT], [t1])
                    t2 = tf()
                    tt(t2[:, :T], psw[:, :T], sinT[:, :T], ALU.mult, [psw, sinT], [t2])
                    ps_release(psw)
                    t3 = tf()
                    tt(t3[:, :T], t1[:, :T], t2[:, :T], ALU.add, [t1, t2], [t3])
                    tt(out_ap, t3[:, :T], rstd[:, :T], ALU.mult, [t3, rstd], [out_reg])
                else:
                    tt(out_ap, qg[:, :T], rstd[:, :T], ALU.mult, [qg, rstd], [out_reg])

            for (is_ctx, t0, T, co) in tiles:
                w = 1 if is_ctx else 0
                src = xc_src if is_ctx else x_src
                kb.dma("sp", xt[:, :, :T], fm(src, t0, t0 + T), [src], [xt])
                if not is_ctx:
                    kb.dma("sp", cosT[:, :T], cos_in.h.ap()[:, t0:t0 + T], [cos_in], [cosT])
                    kb.dma("sp", sinT[:, :T], sin_in.h.ap()[:, t0:t0 + T], [sin_in], [sinT])
                normmod(xt, T, der[:, w, 0, :], der[:, w, 1, :], hT)
                kb.dma("sp", fm(hs, co, co + T), hT[:, :, :T], [hT], [hs])
                if STOP_STAGE == 1.1:
                    kb.barrier(); return
                need_q = not (is_ctx and last)
                if need_q:
                    wbuf, wv = wsA.get(wi); wi += 1
                    for h in range(NQH):
                        pq = ps_alloc()
                        proj(pq, wbuf, wv, h * 128, hT, T)
                        qk_post(pq, T, 0, not is_ctx, qT[:, h, :T], qT)
                        ps_release(pq)
                    kb.dma("sp", fm(qs, co, co + T), qT[:, :, :T], [qT], [qs])
                if STOP_STAGE == 1.2:
                    kb.barrier(); return
                wbuf, wv = wsA.get(wi); wi += 1
                for g in range(NKVH):
                    pk = ps_alloc()
                    proj(pk, wbuf, wv, g * 128, hT, T)
                    qk_post(pk, T, 1, not is_ctx, kT[:, g, :T], kT)
                    ps_release(pk)
                kdst = kcloc if is_ctx else kloc
                kb.dma("sp", kdst.h.ap().rearrange("(g d) t -> d g t", d=128)[:, :, t0:t0 + T], kT[:, :, :T], [kT], [kdst])
                if STOP_STAGE == 1.3:
                    kb.barrier(); return
                for s in range(T // 128):
                    pv = ps_alloc()
                    for kc in range(KC):
                        mm(pv[:, :256], hT[:, kc, s * 128:(s + 1) * 128], wv[:, kc, 256:512], kc == 0, kc == KC - 1,
                           [hT, wbuf], [pv])
                    copy(ACT, vt[:, s, :], pv[:, :256], [pv], [vt])
                    ps_release(pv)
                vdst = vcloc if is_ctx else vloc
                for g in range(NKVH):
                    kb.dma("sp", vdst.h.ap()[g * 128:(g + 1) * 128, :].rearrange("p (b d) -> p b d", d=128)[:, t0 // 128: t0 // 128 + T // 128, :],
                           vt[:, :T // 128, g * 128:(g + 1) * 128], [vt], [vdst])
                if STOP_STAGE == 1.4:
                    kb.barrier(); return
                if not (is_ctx and last):
                    wb1, wv1 = wsA.get(wi); wi += 1
                    wb2, wv2 = wsA.get(wi); wi += 1
                    for j in range(KC):
                        p1 = ps_alloc()
                        proj(p1, wb1, wv1, j * 128, hT, T)
                        p2 = ps_alloc()
                        proj(p2, wb2, wv2, j * 128, hT, T)
                        sg = tf()
                        act(sg[:, :T], p2[:, :T], AF.Sigmoid, [p2], [sg])
                        ps_release(p2)
                        tt(glT[:, j, :T], p1[:, :T], sg[:, :T], ALU.mult, [p1, sg], [glT])
                        ps_release(p1)
                    if is_ctx:
                        kb.dma("sp", fm(glc, HALO, HALO + CTX), glT[:, :, :T], [glT], [glc])
                    else:
                        kb.dma("sp", fm(gls, HALO + t0, HALO + t0 + T), glT[:, :, :T], [glT], [gls])
                        if t0 == 0:
                            copy(DVE, ebuf[:, 0, :, :], glT[:, :, 0:HALO], [glT], [ebuf])
                        if t0 + T == TOK:
                            copy(DVE, ebuf[:, 1, :, :], glT[:, :, T - HALO:T], [glT], [ebuf])
                            kb.dma("sp", eloc.h.ap(), ebuf[:].rearrange("p a k c -> p (a k c)"), [ebuf], [eloc])
            kb.barrier()

        if STOP_STAGE <= 2:
            return
        kb.allgather(kloc.h.ap(), kall.h.ap(), [kloc], [kall])
        kb.allgather(vloc.h.ap(), vall.h.ap(), [vloc], [vall])
        kb.allgather(eloc.h.ap(), eall.h.ap(), [eloc], [eall])
        with contextlib.ExitStack() as ph:
            ee = kb.sb(ph, "x_ee", [128, NCORES, 2, KC * HALO], BF16)
            hal = kb.sb(ph, "x_hal", [128, 2, KC * HALO], F32)
            kb.dma("sp", ee[:].rearrange("p r a c -> p r (a c)"), eall.h.ap().rearrange("(r p) c -> p r c", p=128), [eall], [ee])
            for side in range(2):
                a_src = 1 if side == 0 else 0
                for r in range(NCORES):
                    sc_ap = sel[:, side * 8 + r: side * 8 + r + 1]
                    if r == 0:
                        ts(hal[:, side, :], ee[:, r, a_src, :], sc_ap, None, ALU.mult, None, [ee, sel], [hal])
                    else:
                        stt(hal[:, side, :], ee[:, r, a_src, :], sc_ap, hal[:, side, :], ALU.mult, ALU.add,
                            [ee, sel, hal], [hal])
            copy(DVE, halb[:].rearrange("p a k c -> p a (k c)"), hal[:], [hal], [halb])
            kb.barrier()

        if STOP_STAGE <= 3:
            return
        with contextlib.ExitStack() as ph:
            Kt = kb.sb(ph, "b_Kt", [128, NKEY], BF16)
            Vt = kb.sb(ph, "b_Vt", [128, NKB, 128], BF16)
            QT = [kb.sb(ph, f"b_q{i}", [128, 4, TT], BF16) for i in range(2)]
            PT = [kb.sb(ph, f"b_p{i}", [128, TT], BF16) for i in range(4)]
            OT = [kb.sb(ph, f"b_o{i}", [128, TT], BF16) for i in range(2)]
            S_PS = PS[0:4]
            O_PS = PS[4:6]
            L_PS = PS[6:8]
            cnt = [0, 0]
            qi = 0
            for g in range(NKVH):
                kb.dma("sp", Kt[:, 0:CTX], kcloc.h.ap()[g * 128:(g + 1) * 128, :], [kcloc], [Kt])
                kb.dma("sp", Kt[:, CTX:NKEY].rearrange("p (r t) -> p r t", r=NCORES),
                       kall.h.ap().rearrange("(r g d) t -> d r g t", g=NKVH, d=128)[:, :, g, :], [kall], [Kt])
                Vflat = Vt[:].rearrange("p b d -> p (b d)")
                kb.dma("sp", Vflat[:, 0:CTX], vcloc.h.ap()[g * 128:(g + 1) * 128, :], [vcloc], [Vt])
                kb.dma("sp", Vflat[:, CTX:NKEY].rearrange("p (r t) -> p r t", r=NCORES),
                       vall.h.ap().rearrange("(r g d) t -> d r g t", g=NKVH, d=128)[:, :, g, :], [vall], [Vt])
                for (is_ctx, t0, T, co) in tiles:
                    if is_ctx and last:
                        continue
                    qt = QT[qi % 2]
                    qi += 1
                    kb.dma("sp", qt[:, :, :T],
                           qs.h.ap().rearrange("(h d) t -> d h t", d=128)[:, 4 * g:4 * g + 4, co:co + T], [qs], [qt])
                    nkb = CTX // 128 if is_ctx else NKB
                    for hh in range(4):
                        h = 4 * g + hh
                        ops_ = O_PS[cnt[1] % 2]
                        lps_ = L_PS[cnt[1] % 2]
                        ot = OT[cnt[1] % 2]
                        cnt[1] += 1
                        LA = 2
                        pend = []

                        def qk(kbk):
                            sp_ = S_PS[cnt[0] % 4]
                            pt_ = PT[cnt[0] % 4]
                            cnt[0] += 1
                            mm(sp_[:, :T], Kt[:, kbk * 128:(kbk + 1) * 128], qt[:, hh, :T], True, True, [Kt, qt], [sp_])
                            act(pt_[:, :T], sp_[:, :T], AF.Exp, [sp_], [pt_], scale=ATTN_SCALE)
                            return pt_

                        def pv(kbk, pt_):
                            mm(ops_[:, :T], Vt[:, kbk, :], pt_[:, :T], kbk == 0, kbk == nkb - 1, [Vt, pt_], [ops_], inc=False)
                            mm(lps_[:, :T], ones_bf[:], pt_[:, :T], kbk == 0, kbk == nkb - 1, [ones_bf, pt_], [lps_], inc=True)

                        for kbk in range(nkb + LA):
                            if kbk < nkb:
                                pend.append((kbk, qk(kbk)))
                            if kbk >= LA:
                                pv(*pend.pop(0))
                        rl = tf()
                        recip(rl[:, :T], lps_[:, :T], [lps_], [rl])
                        tt(ot[:, :T], ops_[:, :T], rl[:, :T], ALU.mult, [ops_, rl], [ot])
                        kb.dma("sp", attn_s.h.ap()[h * 128:(h + 1) * 128, co:co + T], ot[:, :T], [ot], [attn_s])
            kb.barrier()

        if STOP_STAGE <= 4:
            return
        with contextlib.ExitStack() as ph:
            xt = kb.sb(ph, "c_xt", [128, KC, TT], F32)
            hT = kb.sb(ph, "c_hT", [128, KC, TT], BF16)
            at = kb.sb(ph, "c_at", [128, KC, TT], BF16)
            acc = kb.sb(ph, "c_acc", [128, KC, TT], F32)
            Y16 = kb.sb(ph, "c_y16", [128, KC, TT], F32)
            BIG = [kb.sb(ph, f"c_big{i}", [128, KC, TT], BF16) for i in range(4)]
            glh = kb.sb(ph, "c_glh", [128, KC, TT + 2 * HALO], BF16)
            gvraw = [kb.sb(ph, f"c_gvraw{i}", [128, D], F32) for i in range(2)]
            gstat = kb.sb(ph, "c_gstat", [128, 8], F32)
            dg = [kb.sb(ph, f"c_dg{i}", [128, CK, 128], BF16) for i in range(2)]
            uT, gvT, mT, zT = BIG

            ctiles = [tl for tl in tiles if not (tl[0] and last)]
            wsC = WStream()
            for (is_ctx, t0, T, co) in ctiles:
                wsC.add(w_attn_o, wsrc(w_attn_o, l, 0, D, 0, D), KC, 1024)
                wsC.add(w_in, wsrc(w_in, l, 0, D, GATE_OFF, GATE_OFF + 1024), KC, 1024)
                wsC.add(w_in, wsrc(w_in, l, 0, D, GU_OFF, GU_OFF + 1024), KC, 1024)
                wsC.add(w_in, wsrc(w_in, l, 0, D, GV_OFF, GV_OFF + 1024), KC, 1024)
                wsC.add(w_gmlp_o, wsrc(w_gmlp_o, l, 0, D, 0, D), KC, 1024)
                wsC.add(w_in, wsrc(w_in, l, 0, D, GATE_OFF + 1024, GATE_OFF + 2048), KC, 1024)
                wsC.add(w_conv_o, wsrc(w_conv_o, l, 0, D, 0, D), KC, 1024)
                wsC.add(w_in, wsrc(w_in, l, 0, D, GATE_OFF + 2048, GATE_OFF + 3072), KC, 1024)
                wsC.add(w_out, wsrc(w_out, l, 0, D, 0, D), KC, 1024)
                for q4 in range(4):
                    wsC.add(w_ff1, wsrc(w_ff1, l, 0, D, q4 * 1024, (q4 + 1) * 1024), KC, 1024)
                for ch in range(2):
                    for kh in range(2):
                        wsC.add(w_ff2, wsrc(w_ff2, l, kh * 2048, (kh + 1) * 2048, ch * 512, (ch + 1) * 512), 16, 512)
            wi = 0

            def gated_acc(pbr, wbg, wvg, j, T, first):
                pg = ps_alloc()
                proj(pg, wbg, wvg, j * 128, hT, T)
                sg = tf()
                act(sg[:, :T], pg[:, :T], AF.Sigmoid, [pg], [sg])
                ps_release(pg)
                if first:
                    tt(acc[:, j, :T], pbr[:, :T], sg[:, :T], ALU.mult, [pbr, sg], [acc])
                else:
                    tmp = tf()
                    tt(tmp[:, :T], pbr[:, :T], sg[:, :T], ALU.mult, [pbr, sg], [tmp])
                    tt(acc[:, j, :T], acc[:, j, :T], tmp[:, :T], ALU.add, [acc, tmp], [acc], eng=POOL)

            for (is_ctx, t0, T, co) in ctiles:
                w = 1 if is_ctx else 0
                src = xc_src if is_ctx else x_src
                NS = T // 128
                kb.dma("sp", xt[:, :, :T], fm(src, t0, t0 + T), [src], [xt])
                kb.dma("sp", hT[:, :, :T], fm(hs, co, co + T), [hs], [hT])
                kb.dma("sp", at[:, :, :T], fm(attn_s, co, co + T), [attn_s], [at])
                gsrc = glc if is_ctx else gls
                c_lo = 0 if ((not is_ctx) and t0 > 0) else HALO
                c_hi = T + 2 * HALO if ((not is_ctx) and t0 + T < TOK) else T + HALO
                kb.dma("sp", glh[:, :, c_lo:c_hi], fm(gsrc, t0 + c_lo, t0 + c_hi), [gsrc], [glh])
                if is_ctx:
                    memset(POOL, glh[:, :, 0:HALO], 0.0, [glh])
                    memset(POOL, glh[:, :, T + HALO:T + 2 * HALO], 0.0, [glh])
                else:
                    if t0 == 0:
                        copy(POOL, glh[:, :, 0:HALO], halb[:, 0, :, :], [halb], [glh])
                    if t0 + T == TOK:
                        copy(POOL, glh[:, :, T + HALO:T + 2 * HALO], halb[:, 1, :, :], [halb], [glh])

                wbo, wvo = wsC.get(wi); wi += 1
                wbg, wvg = wsC.get(wi); wi += 1
                for j in range(KC):
                    pb = ps_alloc()
                    proj(pb, wbo, wvo, j * 128, at, T)
                    gated_acc(pb, wbg, wvg, j, T, True)
                    ps_release(pb)

                if debug and l == 0 and (not is_ctx) and t0 == 0:
                    kb.dma("sp", dbgc.h.ap()[0 * D:1 * D, :].rearrange("(k p) c -> p k c", p=128), acc[:, :, :T], [acc], [dbgc])
                wbu, wvu = wsC.get(wi); wi += 1
                for j in range(KC):
                    pu = ps_alloc()
                    proj(pu, wbu, wvu, j * 128, hT, T)
                    act(uT[:, j, :T], pu[:, :T], AF.Gelu, [pu], [uT])
                    ps_release(pu)
                wbv, wvv = wsC.get(wi); wi += 1
                for s in range(NS):
                    gr = gvraw[s % 2]
                    for half in range(2):
                        pv_ = ps_alloc()
                        for kc in range(KC):
                            mm(pv_[:, :512], hT[:, kc, s * 128:(s + 1) * 128], wvv[:, kc, half * 512:(half + 1) * 512],
                               kc == 0, kc == KC - 1, [hT, wbv], [pv_])
                        act(gr[:, half * 512:(half + 1) * 512], pv_[:, :512], AF.Gelu, [pv_], [gr])
                        ps_release(pv_)
                    junk = tf()
                    for half in range(2):
                        act(junk[:, :512], gr[:, half * 512:(half + 1) * 512], AF.Square, [gr], [junk, gstat],
                            accum_out=gstat[:, half:half + 1])
                    tt(gstat[:, 2:3], gstat[:, 0:1], gstat[:, 1:2], ALU.add, [gstat], [gstat])
                    act(gstat[:, 3:4], gstat[:, 2:3], AF.Sqrt, [gstat], [gstat], bias=EPS, scale=1.0 / D)
                    recip(gstat[:, 4:5], gstat[:, 3:4], [gstat], [gstat])
                    stt(gvT[:].rearrange("p k t -> p (k t)")[:, s * D:(s + 1) * D],
                        gr[:], gstat[:, 4:5], gng[:], ALU.mult, ALU.mult, [gr, gstat, gng], [gvT])
                gv_flat = gvT[:].rearrange("p k t -> p (k t)")
                wbgo, wvgo = wsC.get(wi); wi += 1
                wbg, wvg = wsC.get(wi); wi += 1
                for j in range(KC):
                    gg = j // 2
                    psv = ps_alloc()
                    for s in range(NS):
                        mm(psv[:, s * 128:(s + 1) * 128], gv_flat[:, s * D + j * 128: s * D + (j + 1) * 128], wsT[:, gg, :],
                           True, False, [gvT, wsT], [psv], inc=False)
                        mm(psv[:, s * 128:(s + 1) * 128], onesrow[0:1, :], bsrow[0:1, gg * 128:(gg + 1) * 128],
                           False, True, [onesrow, bsrow], [psv], inc=True)
                    tt(mT[:, j, :T], psv[:, :T], uT[:, j, :T], ALU.mult, [psv, uT], [mT])
                    ps_release(psv)
                for j in range(KC):
                    pb = ps_alloc()
                    proj(pb, wbgo, wvgo, j * 128, mT, T)
                    gated_acc(pb, wbg, wvg, j, T, False)
                    ps_release(pb)

                if debug and l == 0 and (not is_ctx) and t0 == 0:
                    kb.dma("sp", dbgc.h.ap()[1 * D:2 * D, :].rearrange("(k p) c -> p k c", p=128), acc[:, :, :T], [acc], [dbgc])
                sum1 = ps_alloc()
                sum2 = ps_alloc()
                for j in range(KC):
                    dgt = dg[j % 2]
                    for k in range(CK):
                        ts(dgt[:, k, :], ident_bf[:], convw[:, j, k:k + 1], None, ALU.mult, None,
                           [ident_bf, convw], [dgt], eng=POOL)
                    py = ps_alloc()
                    for k in range(CK):
                        mm(py[:, :T], dgt[:, k, :], glh[:, j, k + 1:k + 1 + T], k == 0, k == CK - 1, [dgt, glh], [py])
                    act(Y16[:, j, :T], py[:, :T], AF.Identity, [py, vecs], [Y16], bias=vecs[:, 2, j:j + 1])
                    ps_release(py)
                    yb = tb()
                    act(yb[:, :T], Y16[:, j, :T], AF.Copy, [Y16], [yb])
                    ysq = tb()
                    act(ysq[:, :T], Y16[:, j, :T], AF.Square, [Y16], [ysq])
                    mm(sum1[:, :T], ones_bf[:], yb[:, :T], j == 0, j == KC - 1, [ones_bf, yb], [sum1], inc=True)
                    mm(sum2[:, :T], ones_bf[:], ysq[:, :T], j == 0, j == KC - 1, [ones_bf, ysq], [sum2], inc=True)
                if debug and l == 0 and (not is_ctx) and t0 == 0:
                    kb.dma("sp", dbgc.h.ap()[4 * D:5 * D, :].rearrange("(k p) c -> p k c", p=128), Y16[:, :, :T], [Y16], [dbgc])
                mu = gvraw[0]
                ts(mu[:, :T], sum1[:, :T], 1.0 / D, None, ALU.mult, None, [sum1], [mu])
                ps_release(sum1)
                msq = tf()
                tt(msq[:, :T], mu[:, :T], mu[:, :T], ALU.mult, [mu], [msq])
                var = tf()
                stt(var[:, :T], sum2[:, :T], 1.0 / D, msq[:, :T], ALU.mult, ALU.subtract, [sum2, msq], [var])
                ps_release(sum2)
                stdc = tf()
                act(stdc[:, :T], var[:, :T], AF.Sqrt, [var], [stdc], bias=EPS, scale=1.0)
                rstdc = gvraw[1]
                recip(rstdc[:, :T], stdc[:, :T], [stdc], [rstdc])
                for j in range(KC):
                    t1 = tf()
                    tt(t1[:, :T], Y16[:, j, :T], mu[:, :T], ALU.subtract, [Y16, mu], [t1])
                    t2 = tf()
                    tt(t2[:, :T], t1[:, :T], rstdc[:, :T], ALU.mult, [t1, rstdc], [t2])
                    act(zT[:, j, :T], t2[:, :T], AF.Silu, [t2, vecs], [zT],
                        bias=vecs[:, 4, j:j + 1], scale=vecs[:, 3, j:j + 1])
                wbco, wvco = wsC.get(wi); wi += 1
                wbg, wvg = wsC.get(wi); wi += 1
                for j in range(KC):
                    pb = ps_alloc()
                    proj(pb, wbco, wvco, j * 128, zT, T)
                    gated_acc(pb, wbg, wvg, j, T, False)
                    ps_release(pb)

                if debug and l == 0 and (not is_ctx) and t0 == 0:
                    kb.dma("sp", dbgc.h.ap()[2 * D:3 * D, :].rearrange("(k p) c -> p k c", p=128), acc[:, :, :T], [acc], [dbgc])
                mb = at
                for j in range(KC):
                    copy(ACT, mb[:, j, :T], acc[:, j, :T], [acc], [mb])
                wbw, wvw = wsC.get(wi); wi += 1
                for j in range(KC):
                    pb = ps_alloc()
                    proj(pb, wbw, wvw, j * 128, mb, T)
                    copy(ACT, Y16[:, j, :T], pb[:, :T], [pb], [Y16])
                    ps_release(pb)
                postnorm_res(Y16, T, der[:, w, 2, :], xt)

                if debug and l == 0 and (not is_ctx) and t0 == 0:
                    kb.dma("sp", dbgc.h.ap()[3 * D:4 * D, :].rearrange("(k p) c -> p k c", p=128), xt[:, :, :T], [xt], [dbgc])
                h2 = hT
                normmod(xt, T, der[:, w, 3, :], der[:, w, 4, :], h2)
                for q4 in range(4):
                    wb1, wv1 = wsC.get(wi); wi += 1
                    hid = BIG[q4]
                    for jj in range(KC):
                        pb = ps_alloc()
                        proj(pb, wb1, wv1, jj * 128, h2, T)
                        r = tf()
                        act(r[:, :T], pb[:, :T], AF.Relu, [pb], [r])
                        ps_release(pb)
                        tt(hid[:, jj, :T], r[:, :T], r[:, :T], ALU.mult, [r], [hid])
                for ch in range(2):
                    wb2a, wv2a = wsC.get(wi); wi += 1
                    wb2b, wv2b = wsC.get(wi); wi += 1
                    for jl in range(4):
                        j = ch * 4 + jl
                        pb = ps_alloc()
                        for kc in range(32):
                            wb2, wv2 = (wb2a, wv2a) if kc < 16 else (wb2b, wv2b)
                            mm(pb[:, :T], wv2[:, kc % 16, jl * 128:(jl + 1) * 128], BIG[kc // 8][:, kc % 8, :T],
                               kc == 0, kc == 31, [wb2, BIG[kc // 8]], [pb])
                        copy(ACT, Y16[:, j, :T], pb[:, :T], [pb], [Y16])
                        ps_release(pb)
                postnorm_res(Y16, T, der[:, w, 5, :], xt)
                dst = xcres if is_ctx else x_dst
                kb.dma("sp", fm(dst, t0, t0 + T), xt[:, :, :T], [xt], [dst])
            kb.barrier()
    kb.barrier()


def _rope_tables():
    rows = SEQ // GRID_W
    row = np.repeat(np.arange(rows), GRID_W).astype(np.float32)
    col = np.tile(np.arange(GRID_W), rows).astype(np.float32)
    nf = HD // 4
    inv = (np.float32(10000.0) ** (-(np.arange(nf, dtype=np.float32) / np.float32(nf)))).astype(np.float32)
    ang = np.concatenate([row[:, None] * inv, col[:, None] * inv], axis=-1).astype(np.float32)
    cos = np.cos(ang).astype(np.float32)
    sin = np.sin(ang).astype(np.float32)
    cosT = np.concatenate([cos.T, cos.T], axis=0)
    sinT = np.concatenate([-sin.T, sin.T], axis=0)
    return np.ascontiguousarray(cosT), np.ascontiguousarray(sinT)


def _pp(v):
    return np.ascontiguousarray(np.asarray(v, np.float32).reshape(KC, 128).T)


def make_in_maps(x, c, ctx, c_ctx, ada_w, ada_b, mix_pre_g, mix_post_g, w_in, q_norm_g, k_norm_g,
                 w_attn_o, gmlp_norm_g, gmlp_ws, gmlp_bs, w_gmlp_o, conv_w, conv_b, conv_norm_g,
                 conv_norm_b, w_conv_o, w_out, ffn_pre_g, ffn_post_g, w_ff1, w_ff2):
    f = lambda a: np.ascontiguousarray(np.asarray(a, dtype=np.float32))
    x = f(x); ctx = f(ctx)
    xT = np.ascontiguousarray(x[0].T)
    ctxT = np.ascontiguousarray(ctx[0].T)
    cT = np.stack([_pp(f(c)[0]), _pp(f(c_ctx))], axis=-1).reshape(128, KC * 2)
    L = DEPTH
    vecs = np.zeros((L, 128, 7, KC), np.float32)
    for l in range(L):
        for i, v in enumerate([mix_pre_g, mix_post_g, conv_b, conv_norm_g, conv_norm_b, ffn_pre_g, ffn_post_g]):
            vecs[l, :, i, :] = _pp(f(v)[l])
    vecs = vecs.reshape(L, 128, 7 * KC)
    adab = np.stack([f(ada_b)[l].reshape(48, 128).T for l in range(L)])
    qkg = np.stack([np.stack([f(q_norm_g)[l], f(k_norm_g)[l]], axis=-1) for l in range(L)])
    gng = np.stack([np.broadcast_to(f(gmlp_norm_g)[l][None, :], (128, D)) for l in range(L)])
    convw = np.stack([f(conv_w)[l].T.reshape(KC, 128, CK).transpose(1, 0, 2).reshape(128, KC * CK) for l in range(L)])
    wsT = np.stack([f(gmlp_ws)[l].transpose(2, 0, 1).reshape(128, 4 * 128) for l in range(L)])
    bsrow = np.stack([f(gmlp_bs)[l].reshape(1, 512) for l in range(L)])
    cosT, sinT = _rope_tables()
    ident = np.eye(128, dtype=np.float32)
    perm = np.zeros((128, 128), np.float32)
    for m in range(128):
        perm[(m + 64) % 128, m] = 1.0
    shared = dict(ctxT=ctxT, cT=np.ascontiguousarray(cT), ada_w=f(ada_w), w_in=f(w_in), w_attn_o=f(w_attn_o),
                  w_gmlp_o=f(w_gmlp_o), w_conv_o=f(w_conv_o), w_out=f(w_out), w_ff1=f(w_ff1), w_ff2=f(w_ff2),
                  vecs=np.ascontiguousarray(vecs), adab=np.ascontiguousarray(adab), qkg=np.ascontiguousarray(qkg),
                  gng=np.ascontiguousarray(gng), convw=np.ascontiguousarray(convw), wsT=np.ascontiguousarray(wsT),
                  bsrow=np.ascontiguousarray(bsrow), ident=ident, perm=perm)
    maps = []
    for r in range(NCORES):
        m = dict(shared)
        m["xT"] = np.ascontiguousarray(xT[:, r * TOK:(r + 1) * TOK])
        m["cosT"] = np.ascontiguousarray(cosT[:, r * TOK:(r + 1) * TOK])
        m["sinT"] = np.ascontiguousarray(sinT[:, r * TOK:(r + 1) * TOK])
        sel = np.zeros((128, 16), np.float32)
        if r > 0:
            sel[:, r - 1] = 1.0
        if r < NCORES - 1:
            sel[:, 8 + r + 1] = 1.0
        m["sel"] = sel
        maps.append(m)
    return maps


_NC_CACHE = {}


def kernel(**inputs):
    maps = make_in_maps(**inputs)
    if "nc" not in _NC_CACHE:
        _NC_CACHE["nc"] = build_program()
    nc = _NC_CACHE["nc"]
    res = run_bass_kernel_spmd(nc, maps, core_ids=list(range(NCORES)))
    outT = np.concatenate([np.asarray(res.results[r]["outT"]) for r in range(NCORES)], axis=1)
    return np.ascontiguousarray(outT.T)[None, :, :].astype(np.float32)
```

```python
import math
import contextlib
import numpy as np
import concourse.bass as bass
import concourse.mybir as mybir
from concourse.bass_utils import run_bass_kernel_spmd

F32 = mybir.dt.float32
BF16 = mybir.dt.bfloat16
AF = mybir.ActivationFunctionType
ALU = mybir.AluOpType

NCORES = 8
D = 1024
SEQ = 16384
TOK = SEQ // NCORES
CTX = 256
DEPTH = 2
GRID_W = 64
HD = 128
NQH = 8
NKVH = 2
KC = D // 128
EPS = 1e-6
ATTN_SCALE = 1.0 / math.sqrt(HD)
Q_OFF, K_OFF, V_OFF, GU_OFF, GV_OFF, CG_OFF = 0, 1024, 1280, 1536, 2560, 3584
GATE_OFF = 5632
IN_COLS = 8704
DFF = 4096
CK = 31
TT = 512
NKEY = CTX + SEQ
NKB = NKEY // 128
HALO = 16
Q_SUB = 99
STOP_STAGE = 99


class Reg:
    __slots__ = ("name", "w", "r", "excl")

    def __init__(self, name):
        self.name = name
        self.w = {}
        self.r = {}
        self.excl = False


class Ten:
    def __init__(self, h, name):
        self.h = h
        self.reg = Reg(name)

    def __getitem__(self, k):
        return self.h[k]


class Eng:
    def __init__(self, name, h, sem):
        self.name = name
        self.h = h
        self.sem = sem
        self.cnt = 0
        self.waited = {}


class KB:
    def __init__(self, nc, es):
        self.nc = nc
        self.es = es
        self.sems = {}
        mk = lambda n: es.enter_context(nc.semaphore(n))
        self.pe = Eng("pe", nc.tensor, mk("s_pe"))
        self.act = Eng("act", nc.scalar, mk("s_act"))
        self.dve = Eng("dve", nc.vector, mk("s_dve"))
        self.pool = Eng("pool", nc.gpsimd, mk("s_pool"))
        self.sp = Eng("sp", nc.sync, None)
        self.engs = [self.pe, self.act, self.dve, self.pool, self.sp]
        for e in self.engs:
            if e.sem is not None:
                self.sems[id(e.sem)] = e.sem
        self.dsem = {}
        for q, n in (("sp", 12), ("pool", 6)):
            lst = []
            for i in range(n):
                s = mk(f"d_{q}{i}")
                self.sems[id(s)] = s
                lst.append([s, 0])
            self.dsem[q] = [lst, 0]
        self.ccsem = [mk("s_cc"), 0]
        self.sems[id(self.ccsem[0])] = self.ccsem[0]
        self.n_inst = 0

    def _deps(self, eng, reads, writes):
        deps = {}
        for R in reads:
            for k, v in R.w.items():
                if deps.get(k, 0) < v:
                    deps[k] = v
            if R.excl:
                for k, v in R.r.items():
                    if deps.get(k, 0) < v:
                        deps[k] = v
        for R in writes:
            for dct in (R.w, R.r):
                for k, v in dct.items():
                    if deps.get(k, 0) < v:
                        deps[k] = v
        for k, v in deps.items():
            if eng is self.pe and k == id(self.pe.sem):
                continue
            if eng.waited.get(k, 0) < v:
                eng.h.wait_ge(self.sems[k], v)
                eng.waited[k] = v
                self.n_inst += 1

    def _mark(self, reads, writes, key, val):
        for R in reads:
            if R.r.get(key, 0) < val:
                R.r[key] = val
        for R in writes:
            if R.w.get(key, 0) < val:
                R.w[key] = val

    @staticmethod
    def _regs(lst):
        return [x.reg if isinstance(x, Ten) else x for x in lst]

    def op(self, eng, fn, reads, writes, inc=True):
        reads = self._regs(reads)
        writes = self._regs(writes)
        self._deps(eng, reads, writes)
        ins = fn()
        self.n_inst += 1
        if inc:
            eng.cnt += 1
            ins.then_inc(eng.sem, 1)
            val = eng.cnt
        else:
            val = eng.cnt + 1
        self._mark(reads, writes, id(eng.sem), val)
        return ins

    def dma(self, q, out, in_, reads, writes):
        eng = self.sp if q == "sp" else self.pool
        reads = self._regs(reads)
        writes = self._regs(writes)
        lst, idx = self.dsem[q]
        ent = lst[idx]
        self.dsem[q][1] = (idx + 1) % len(lst)
        sem, val = ent
        if val > 0 and eng.waited.get(id(sem), 0) < val:
            eng.h.wait_ge(sem, val)
            eng.waited[id(sem)] = val
        self._deps(eng, reads, writes)
        eng.h.dma_start(out=out, in_=in_).then_inc(sem, 16)
        self.n_inst += 1
        ent[1] = val + 16
        self._mark(reads, writes, id(sem), val + 16)

    def allgather(self, in_ap, out_ap, reads, writes):
        eng = self.pool
        reads = self._regs(reads)
        writes = self._regs(writes)
        self._deps(eng, reads, writes)
        sem = self.ccsem[0]
        eng.h.collective_compute("AllGather", ALU.bypass, replica_groups=[list(range(NCORES))],
                                 ins=[in_ap.opt()], outs=[out_ap.opt()]).then_inc(sem)
        self.ccsem[1] += 1
        self._mark(reads, writes, id(sem), self.ccsem[1])
        eng.h.wait_ge(sem, self.ccsem[1])
        eng.waited[id(sem)] = self.ccsem[1]

    def barrier(self):
        targets = {}
        for e in (self.pe, self.act, self.dve, self.pool):
            if e.cnt > 0:
                targets[id(e.sem)] = e.cnt
        for q in ("sp", "pool"):
            for s, v in self.dsem[q][0]:
                if v > 0:
                    targets[id(s)] = v
        if self.ccsem[1] > 0:
            targets[id(self.ccsem[0])] = self.ccsem[1]
        for e in self.engs:
            for k, v in targets.items():
                if e is self.pe and k == id(self.pe.sem):
                    continue
                if e.waited.get(k, 0) < v:
                    e.h.wait_ge(self.sems[k], v)
                    e.waited[k] = v

    def sb(self, stack, name, shape, dt):
        self.uid = getattr(self, "uid", 0) + 1
        name = f"sb{self.uid}_{name}"
        return Ten(stack.enter_context(self.nc.sbuf_tensor(name, shape, dt)), name)

    def dram(self, name, shape, dt, kind="Internal"):
        return Ten(self.nc.dram_tensor(name, shape, dt, kind=kind), name)


def build_program(debug=False, n_layers=DEPTH):
    nc = bass.Bass("TRN2", target_bir_lowering=False)
    es = contextlib.ExitStack()
    with es:
        _emit(nc, es, debug, n_layers)
    return nc


def _emit(nc, es, debug, n_layers):
    kb = KB(nc, es)
    PE, ACT, DVE, POOL = kb.pe, kb.act, kb.dve, kb.pool
    L = DEPTH

    def ext_in(name, shape, dt=F32):
        return kb.dram(name, shape, dt, kind="ExternalInput")

    xT_in = ext_in("xT", [D, TOK])
    ctxT_in = ext_in("ctxT", [D, CTX])
    cT_in = ext_in("cT", [128, KC * 2])
    ada_w = ext_in("ada_w", [L, D, 6 * D])
    w_in = ext_in("w_in", [L, D, IN_COLS])
    w_attn_o = ext_in("w_attn_o", [L, D, D])
    w_gmlp_o = ext_in("w_gmlp_o", [L, D, D])
    w_conv_o = ext_in("w_conv_o", [L, D, D])
    w_out = ext_in("w_out", [L, D, D])
    w_ff1 = ext_in("w_ff1", [L, D, DFF])
    w_ff2 = ext_in("w_ff2", [L, DFF, D])
    NV = 7
    vecs_in = ext_in("vecs", [L, 128, NV * KC])
    adab_in = ext_in("adab", [L, 128, 48])
    qkg_in = ext_in("qkg", [L, 128, 2])
    gng_in = ext_in("gng", [L, 128, D])
    convw_in = ext_in("convw", [L, 128, KC * CK])
    wsT_in = ext_in("wsT", [L, 128, 4 * 128])
    bsrow_in = ext_in("bsrow", [L, 1, 512])
    cos_in = ext_in("cosT", [128, TOK])
    sin_in = ext_in("sinT", [128, TOK])
    ident_in = ext_in("ident", [128, 128])
    perm_in = ext_in("perm", [128, 128])
    sel_in = ext_in("sel", [128, 16])
    out_kind = "ExternalOutput"
    outT = kb.dram("outT", [D, TOK], F32, kind=out_kind)

    dbg_kind = "ExternalOutput" if debug else "Internal"
    NT = TOK + CTX
    xres = kb.dram("xres", [D, TOK], F32, kind=dbg_kind)
    xcres = kb.dram("xcres", [D, CTX], F32, kind=dbg_kind)
    hs = kb.dram("hs", [D, NT], BF16, kind=dbg_kind)
    qs = kb.dram("qs", [D, NT], BF16, kind=dbg_kind)
    attn_s = kb.dram("attn_s", [D, NT], BF16, kind=dbg_kind)
    kloc = kb.dram("kloc", [NKVH * 128, TOK], BF16)
    vloc = kb.dram("vloc", [NKVH * 128, TOK], BF16)
    kall = kb.dram("kall", [NCORES * NKVH * 128, TOK], BF16)
    vall = kb.dram("vall", [NCORES * NKVH * 128, TOK], BF16)
    kcloc = kb.dram("kcloc", [NKVH * 128, CTX], BF16, kind=dbg_kind)
    vcloc = kb.dram("vcloc", [NKVH * 128, CTX], BF16, kind=dbg_kind)
    GLW = HALO + TOK + HALO
    gls = kb.dram("gls", [D, GLW], BF16, kind=dbg_kind)
    GCW = HALO + CTX + HALO
    glc = kb.dram("glc", [D, GCW], BF16)
    eloc = kb.dram("eloc", [128, 2 * KC * HALO], BF16)
    eall = kb.dram("eall", [NCORES * 128, 2 * KC * HALO], BF16)

    dbgc = kb.dram("dbgc", [5 * D, TT], F32, kind=dbg_kind)

    def fm(t, c0, c1):
        return t.h.ap().rearrange("(k p) c -> p k c", p=128)[:, :, c0:c1]

    P = es
    ones_bf = kb.sb(P, "ones_bf", [128, 128], BF16)
    ident_bf = kb.sb(P, "ident_bf", [128, 128], BF16)
    perm_bf = kb.sb(P, "perm_bf", [128, 128], BF16)
    onesrow = kb.sb(P, "onesrow", [1, 128], BF16)
    sel = kb.sb(P, "sel", [128, 16], F32)
    sc = kb.sb(P, "sc", [128, KC * 2], F32)
    ebuf = kb.sb(P, "ebuf", [128, 2, KC, HALO], BF16)
    halb = kb.sb(P, "halb", [128, 2, KC, HALO], BF16)
    WB = [kb.sb(P, f"wb{i}", [128, 8192], BF16) for i in range(3)]
    NTF, NTB = 6, 5
    TF = [kb.sb(P, f"tf{i}", [128, TT], F32) for i in range(NTF)]
    TB = [kb.sb(P, f"tb{i}", [128, TT], BF16) for i in range(NTB)]
    RS = [kb.sb(P, f"rs{i}", [128, TT], F32) for i in range(2)]
    rsi = [0]
    tfi = [0]
    tbi = [0]

    def tf():
        t = TF[tfi[0] % NTF]
        tfi[0] += 1
        return t

    def tb():
        t = TB[tbi[0] % NTB]
        tbi[0] += 1
        return t

    PS = [Ten(es.enter_context(nc.psum_tensor(f"ps{i}", [128, 512], F32)), f"ps{i}") for i in range(8)]
    for _p in PS:
        _p.reg.excl = True
    ps_free = list(range(8))

    def ps_alloc():
        assert ps_free, "out of PSUM banks"
        return PS[ps_free.pop(0)]

    def ps_release(t):
        ps_free.append(PS.index(t))

    vecs = kb.sb(P, "vecs", [128, NV, KC], F32)
    adab = kb.sb(P, "adab", [128, 48], F32)
    qkg = kb.sb(P, "qkg", [128, 2], F32)
    gng = kb.sb(P, "gng", [128, D], F32)
    convw = kb.sb(P, "convw", [128, KC, CK], F32)
    wsT = kb.sb(P, "wsT", [128, 4, 128], BF16)
    bsrow = kb.sb(P, "bsrow", [1, 512], BF16)
    modv = kb.sb(P, "modv", [128, 48, 2], F32)
    der = kb.sb(P, "der", [128, 2, 6, KC], F32)

    def act(out, in_, func, reads, writes, bias=None, scale=None, accum_out=None):
        kw = {}
        if bias is not None:
            kw["bias"] = bias
        if scale is not None:
            kw["scale"] = scale
        if accum_out is not None:
            kw["accum_out"] = accum_out
        return kb.op(ACT, lambda: nc.scalar.activation(out=out, in_=in_, func=func, **kw), reads, writes)

    def tt(out, in0, in1, op, reads, writes, eng=None):
        e = eng or DVE
        return kb.op(e, lambda: e.h.tensor_tensor(out=out, in0=in0, in1=in1, op=op), reads, writes)

    def ts(out, in0, s1, s2, op0, op1, reads, writes, eng=None):
        e = eng or DVE
        if s2 is None:
            return kb.op(e, lambda: e.h.tensor_scalar(out=out, in0=in0, scalar1=s1, scalar2=None, op0=op0), reads, writes)
        return kb.op(e, lambda: e.h.tensor_scalar(out=out, in0=in0, scalar1=s1, scalar2=s2, op0=op0, op1=op1), reads, writes)

    def stt(out, in0, scalar, in1, op0, op1, reads, writes):
        return kb.op(DVE, lambda: nc.vector.scalar_tensor_tensor(out=out, in0=in0, scalar=scalar, in1=in1, op0=op0, op1=op1), reads, writes)

    def recip(out, in_, reads, writes):
        return kb.op(DVE, lambda: nc.vector.reciprocal(out=out, in_=in_), reads, writes)

    def copy(eng, out, in_, reads, writes):
        if eng is ACT:
            return act(out, in_, AF.Copy, reads, writes)
        return kb.op(eng, lambda: eng.h.tensor_copy(out=out, in_=in_), reads, writes)

    def memset(eng, ap, val, writes):
        return kb.op(eng, lambda: eng.h.memset(ap, val), [], writes)

    def mm(out, lhsT, rhs, start, stop, reads, writes, inc=None):
        return kb.op(PE, lambda: nc.tensor.matmul(out, lhsT, rhs, start=start, stop=stop), reads, writes,
                     inc=(stop if inc is None else inc))

    memset(DVE, ones_bf[:], 1.0, [ones_bf])
    memset(DVE, onesrow[:], 1.0, [onesrow])
    kb.dma("pool", ident_bf[:], ident_in.h.ap(), [ident_in], [ident_bf])
    kb.dma("pool", perm_bf[:], perm_in.h.ap(), [perm_in], [perm_bf])
    kb.dma("sp", sel[:], sel_in.h.ap(), [sel_in], [sel])
    ctmp = kb.sb(P, "ctmp", [128, KC * 2], F32)
    kb.dma("sp", ctmp[:], cT_in.h.ap(), [cT_in], [ctmp])
    act(sc[:], ctmp[:], AF.Silu, [ctmp], [sc])

    class WStream:
        def __init__(self):
            self.specs = []
            self.issued = 0
            self.slot = 0

        def add(self, src_ten, src_ap, kc, cols, cast=True):
            self.specs.append((src_ten, src_ap, kc, cols, cast))
            return len(self.specs) - 1

        def _issue(self, j):
            src_ten, src_ap, kc, cols, cast = self.specs[j]
            wbuf = WB[j % 3]
            if cast:
                dst = wbuf[:, 0:kc * cols].rearrange("p (k c) -> p k c", k=kc)
                kb.dma("pool", dst, src_ap, [src_ten], [wbuf])
            else:
                dst = wbuf[:].bitcast(F32)[:, 0:kc * cols].rearrange("p (k c) -> p k c", k=kc)
                kb.dma("sp", dst, src_ap, [src_ten], [wbuf])

        def get(self, j):
            while self.issued < min(j + 2, len(self.specs)):
                self._issue(self.issued)
                self.issued += 1
            src_ten, src_ap, kc, cols, cast = self.specs[j]
            wbuf = WB[j % 3]
            if cast:
                view = wbuf[:, 0:kc * cols].rearrange("p (k c) -> p k c", k=kc)
            else:
                view = wbuf[:].bitcast(F32)[:, 0:kc * cols].rearrange("p (k c) -> p k c", k=kc)
            return wbuf, view

    def wsrc(t, l, r0, r1, c0, c1):
        return t.h.ap()[l, r0:r1, c0:c1].rearrange("(k p) c -> p k c", p=128)

    def rms_stats(src_ap_fn, src_regs, nch, T, denom):
        ss = ps_alloc()
        for kc in range(nch):
            sq = tb()
            act(sq[:, :T], src_ap_fn(kc), AF.Square, src_regs, [sq])
            mm(ss[:, :T], ones_bf[:], sq[:, :T], kc == 0, kc == nch - 1, [ones_bf, sq], [ss], inc=True)
        std = tf()
        act(std[:, :T], ss[:, :T], AF.Sqrt, [ss], [std], bias=EPS, scale=1.0 / denom)
        ps_release(ss)
        rstd = RS[rsi[0] % 2]
        rsi[0] += 1
        recip(rstd[:, :T], std[:, :T], [std], [rstd])
        return rstd

    def normmod(xt, T, A, B, hT):
        rstd = rms_stats(lambda kc: xt[:, kc, :T], [xt], KC, T, float(D))
        for kc in range(KC):
            tmp = tf()
            tt(tmp[:, :T], xt[:, kc, :T], rstd[:, :T], ALU.mult, [xt, rstd], [tmp])
            act(hT[:, kc, :T], tmp[:, :T], AF.Identity, [tmp, der], [hT],
                bias=B[:, kc:kc + 1], scale=A[:, kc:kc + 1])

    def postnorm_res(Y, T, G, xt):
        rstd = rms_stats(lambda kc: Y[:, kc, :T], [Y], KC, T, float(D))
        for kc in range(KC):
            tmp = tf()
            tt(tmp[:, :T], Y[:, kc, :T], rstd[:, :T], ALU.mult, [Y, rstd], [tmp])
            stt(xt[:, kc, :T], tmp[:, :T], G[:, kc:kc + 1], xt[:, kc, :T], ALU.mult, ALU.add,
                [tmp, der, xt], [xt])

    def proj(psT, wbuf, wview, col0, src, T, nkc=KC):
        for kc in range(nkc):
            mm(psT[:, :T], wview[:, kc, col0:col0 + 128], src[:, kc, :T], kc == 0, kc == nkc - 1,
               [wbuf, src], [psT])

    for l in range(n_layers):
        last = (l == DEPTH - 1)
        x_src = xT_in if l == 0 else xres
        xc_src = ctxT_in if l == 0 else xcres
        x_dst = outT if last else xres

        kb.dma("sp", vecs[:].rearrange("p v k -> p (v k)"), vecs_in.h.ap()[l], [vecs_in], [vecs])
        kb.dma("sp", adab[:], adab_in.h.ap()[l], [adab_in], [adab])
        kb.dma("sp", qkg[:], qkg_in.h.ap()[l], [qkg_in], [qkg])
        kb.dma("sp", gng[:], gng_in.h.ap()[l], [gng_in], [gng])
        kb.dma("sp", convw[:].rearrange("p k c -> p (k c)"), convw_in.h.ap()[l], [convw_in], [convw])
        kb.dma("pool", wsT[:].rearrange("p g q -> p (g q)"), wsT_in.h.ap()[l], [wsT_in], [wsT])
        kb.dma("pool", bsrow[:], bsrow_in.h.ap()[l], [bsrow_in], [bsrow])

        wsm = WStream()
        for gi in range(12):
            wsm.add(ada_w, wsrc(ada_w, l, 0, D, gi * 512, (gi + 1) * 512), KC, 512, cast=False)
        for gi in range(12):
            wbuf, wv = wsm.get(gi)
            pm = ps_alloc()
            for j in range(4):
                for kc in range(KC):
                    mm(pm[:, j * 2:j * 2 + 2], wv[:, kc, j * 128:(j + 1) * 128], sc[:, kc * 2:kc * 2 + 2],
                       kc == 0, kc == KC - 1, [wbuf, sc], [pm])
            for j in range(4):
                cg = gi * 4 + j
                ts(modv[:, cg, :], pm[:, j * 2:j * 2 + 2], adab[:, cg:cg + 1], None, ALU.add, None,
                   [pm, adab], [modv])
            ps_release(pm)
        for w in range(2):
            stt(der[:, w, 0, :], modv[:, 8:16, w], 1.0, vecs[:, 0, :], ALU.add, ALU.mult, [modv, vecs], [der])
            copy(DVE, der[:, w, 1, :], modv[:, 0:8, w], [modv], [der])
            tt(der[:, w, 2, :], modv[:, 16:24, w], vecs[:, 1, :], ALU.mult, [modv, vecs], [der])
            stt(der[:, w, 3, :], modv[:, 32:40, w], 1.0, vecs[:, 5, :], ALU.add, ALU.mult, [modv, vecs], [der])
            copy(DVE, der[:, w, 4, :], modv[:, 24:32, w], [modv], [der])
            tt(der[:, w, 5, :], modv[:, 40:48, w], vecs[:, 6, :], ALU.mult, [modv, vecs], [der])
        kb.barrier()
        if STOP_STAGE <= 1:
            return

        tiles = [(False, i * TT, TT, i * TT) for i in range(TOK // TT)] + [(True, 0, CTX, TOK)]

        with contextlib.ExitStack() as ph:
            xt = kb.sb(ph, "a_xt", [128, KC, TT], F32)
            hT = kb.sb(ph, "a_hT", [128, KC, TT], BF16)
            glT = kb.sb(ph, "a_glT", [128, KC, TT], BF16)
            qT = kb.sb(ph, "a_qT", [128, NQH, TT], BF16)
            kT = kb.sb(ph, "a_kT", [128, NKVH, TT], BF16)
            vt = kb.sb(ph, "a_vt", [128, TT // 128, NKVH * 128], BF16)
            cosT = kb.sb(ph, "a_cos", [128, TT], F32)
            sinT = kb.sb(ph, "a_sin", [128, TT], F32)

            wsA = WStream()
            for (is_ctx, t0, T, co) in tiles:
                need_q = not (is_ctx and last)
                if need_q:
                    wsA.add(w_in, wsrc(w_in, l, 0, D, Q_OFF, Q_OFF + 1024), KC, 1024)
                wsA.add(w_in, wsrc(w_in, l, 0, D, K_OFF, K_OFF + 512), KC, 512)
                if not (is_ctx and last):
                    wsA.add(w_in, wsrc(w_in, l, 0, D, CG_OFF, CG_OFF + 1024), KC, 1024)
                    wsA.add(w_in, wsrc(w_in, l, 0, D, CG_OFF + 1024, CG_OFF + 2048), KC, 1024)
            wi = 0

            def qk_post(pq, T, gcol, rope, out_ap, out_reg):
                if Q_SUB <= 0:
                    return
                sq = tb()
                act(sq[:, :T], pq[:, :T], AF.Square, [pq], [sq])
                if Q_SUB <= 1:
                    return
                qg = tb()
                ts(qg[:, :T], pq[:, :T], qkg[:, gcol:gcol + 1], None, ALU.mult, None, [pq, qkg], [qg])
                if Q_SUB <= 2:
                    return
                ss = ps_alloc()
                mm(ss[:, :T], ones_bf[:], sq[:, :T], True, True, [ones_bf, sq], [ss])
                std = tf()
                act(std[:, :T], ss[:, :T], AF.Sqrt, [ss], [std], bias=EPS, scale=1.0 / HD)
                ps_release(ss)
                rstd = tf()
                recip(rstd[:, :T], std[:, :T], [std], [rstd])
                if Q_SUB <= 3:
                    return
                if rope:
                    psw = ps_alloc()
                    mm(psw[:, :T], perm_bf[:], qg[:, :T], True, True, [perm_bf, qg], [psw])
                    if Q_SUB <= 4:
                        ps_release(psw)
                        return
                    t1 = tf()
                    tt(t1[:, :T], qg[:, :T], cosT[:, :T], ALU.mult, [qg, cosT], [t1])
                    t2 = tf()
                    tt(t2[:, :T], psw[:, :T], sinT[:, :T], ALU.mult, [psw, sinT], [t2])
                    ps_release(psw)
                    t3 = tf()
                    tt(t3[:, :T], t1[:, :T], t2[:, :T], ALU.add, [t1, t2], [t3])
                    tt(out_ap, t3[:, :T], rstd[:, :T], ALU.mult, [t3, rstd], [out_reg])
                else:
                    tt(out_ap, qg[:, :T], rstd[:, :T], ALU.mult, [qg, rstd], [out_reg])

            for (is_ctx, t0, T, co) in tiles:
                w = 1 if is_ctx else 0
                src = xc_src if is_ctx else x_src
                kb.dma("sp", xt[:, :, :T], fm(src, t0, t0 + T), [src], [xt])
                if not is_ctx:
                    kb.dma("sp", cosT[:, :T], cos_in.h.ap()[:, t0:t0 + T], [cos_in], [cosT])
                    kb.dma("sp", sinT[:, :T], sin_in.h.ap()[:, t0:t0 + T], [sin_in], [sinT])
                normmod(xt, T, der[:, w, 0, :], der[:, w, 1, :], hT)
                kb.dma("sp", fm(hs, co, co + T), hT[:, :, :T], [hT], [hs])
                if STOP_STAGE == 1.1:
                    kb.barrier(); return
                need_q = not (is_ctx and last)
                if need_q:
                    wbuf, wv = wsA.get(wi); wi += 1
                    for h in range(NQH):
                        pq = ps_alloc()
                        proj(pq, wbuf, wv, h * 128, hT, T)
                        qk_post(pq, T, 0, not is_ctx, qT[:, h, :T], qT)
                        ps_release(pq)
                    kb.dma("sp", fm(qs, co, co + T), qT[:, :, :T], [qT], [qs])
                if STOP_STAGE == 1.2:
                    kb.barrier(); return
                wbuf, wv = wsA.get(wi); wi += 1
                for g in range(NKVH):
                    pk = ps_alloc()
                    proj(pk, wbuf, wv, g * 128, hT, T)
                    qk_post(pk, T, 1, not is_ctx, kT[:, g, :T], kT)
                    ps_release(pk)
                kdst = kcloc if is_ctx else kloc
                kb.dma("sp", kdst.h.ap().rearrange("(g d) t -> d g t", d=128)[:, :, t0:t0 + T], kT[:, :, :T], [kT], [kdst])
                if STOP_STAGE == 1.3:
                    kb.barrier(); return
                for s in range(T // 128):
                    pv = ps_alloc()
                    for kc in range(KC):
                        mm(pv[:, :256], hT[:, kc, s * 128:(s + 1) * 128], wv[:, kc, 256:512], kc == 0, kc == KC - 1,
                           [hT, wbuf], [pv])
                    copy(ACT, vt[:, s, :], pv[:, :256], [pv], [vt])
                    ps_release(pv)
                vdst = vcloc if is_ctx else vloc
                for g in range(NKVH):
                    kb.dma("sp", vdst.h.ap()[g * 128:(g + 1) * 128, :].rearrange("p (b d) -> p b d", d=128)[:, t0 // 128: t0 // 128 + T // 128, :],
                           vt[:, :T // 128, g * 128:(g + 1) * 128], [vt], [vdst])
                if STOP_STAGE == 1.4:
                    kb.barrier(); return
                if not (is_ctx and last):
                    wb1, wv1 = wsA.get(wi); wi += 1
                    wb2, wv2 = wsA.get(wi); wi += 1
                    for j in range(KC):
                        p1 = ps_alloc()
                        proj(p1, wb1, wv1, j * 128, hT, T)
                        p2 = ps_alloc()
                        proj(p2, wb2, wv2, j * 128, hT, T)
                        sg = tf()
                        act(sg[:, :T], p2[:, :T], AF.Sigmoid, [p2], [sg])
                        ps_release(p2)
                        tt(glT[:, j, :T], p1[:, :T], sg[:, :T], ALU.mult, [p1, sg], [glT])
                        ps_release(p1)
                    if is_ctx:
                        kb.dma("sp", fm(glc, HALO, HALO + CTX), glT[:, :, :T], [glT], [glc])
                    else:
                        kb.dma("sp", fm(gls, HALO + t0, HALO + t0 + T), glT[:, :, :T], [glT], [gls])
                        if t0 == 0:
                            copy(DVE, ebuf[:, 0, :, :], glT[:, :, 0:HALO], [glT], [ebuf])
                        if t0 + T == TOK:
                            copy(DVE, ebuf[:, 1, :, :], glT[:, :, T - HALO:T], [glT], [ebuf])
                            kb.dma("sp", eloc.h.ap(), ebuf[:].rearrange("p a k c -> p (a k c)"), [ebuf], [eloc])
            kb.barrier()

        if STOP_STAGE <= 2:
            return
        kb.allgather(kloc.h.ap(), kall.h.ap(), [kloc], [kall])
        kb.allgather(vloc.h.ap(), vall.h.ap(), [vloc], [vall])
        kb.allgather(eloc.h.ap(), eall.h.ap(), [eloc], [eall])
        with contextlib.ExitStack() as ph:
            ee = kb.sb(ph, "x_ee", [128, NCORES, 2, KC * HALO], BF16)
            hal = kb.sb(ph, "x_hal", [128, 2, KC * HALO], F32)
            kb.dma("sp", ee[:].rearrange("p r a c -> p r (a c)"), eall.h.ap().rearrange("(r p) c -> p r c", p=128), [eall], [ee])
            for side in range(2):
                a_src = 1 if side == 0 else 0
                for r in range(NCORES):
                    sc_ap = sel[:, side * 8 + r: side * 8 + r + 1]
                    if r == 0:
                        ts(hal[:, side, :], ee[:, r, a_src, :], sc_ap, None, ALU.mult, None, [ee, sel], [hal])
                    else:
                        stt(hal[:, side, :], ee[:, r, a_src, :], sc_ap, hal[:, side, :], ALU.mult, ALU.add,
                            [ee, sel, hal], [hal])
            copy(DVE, halb[:].rearrange("p a k c -> p a (k c)"), hal[:], [hal], [halb])
            kb.barrier()

        if STOP_STAGE <= 3:
            return
        with contextlib.ExitStack() as ph:
            Kt = kb.sb(ph, "b_Kt", [128, NKEY], BF16)
            Vt = kb.sb(ph, "b_Vt", [128, NKB, 128], BF16)
            QT = [kb.sb(ph, f"b_q{i}", [128, 4, TT], BF16) for i in range(2)]
            PT = [kb.sb(ph, f"b_p{i}", [128, TT], BF16) for i in range(4)]
            OT = [kb.sb(ph, f"b_o{i}", [128, TT], BF16) for i in range(2)]
            S_PS = PS[0:4]
            O_PS = PS[4:6]
            L_PS = PS[6:8]
            cnt = [0, 0]
            qi = 0
            for g in range(NKVH):
                kb.dma("sp", Kt[:, 0:CTX], kcloc.h.ap()[g * 128:(g + 1) * 128, :], [kcloc], [Kt])
                kb.dma("sp", Kt[:, CTX:NKEY].rearrange("p (r t) -> p r t", r=NCORES),
                       kall.h.ap().rearrange("(r g d) t -> d r g t", g=NKVH, d=128)[:, :, g, :], [kall], [Kt])
                Vflat = Vt[:].rearrange("p b d -> p (b d)")
                kb.dma("sp", Vflat[:, 0:CTX], vcloc.h.ap()[g * 128:(g + 1) * 128, :], [vcloc], [Vt])
                kb.dma("sp", Vflat[:, CTX:NKEY].rearrange("p (r t) -> p r t", r=NCORES),
                       vall.h.ap().rearrange("(r g d) t -> d r g t", g=NKVH, d=128)[:, :, g, :], [vall], [Vt])
                for (is_ctx, t0, T, co) in tiles:
                    if is_ctx and last:
                        continue
                    qt = QT[qi % 2]
                    qi += 1
                    kb.dma("sp", qt[:, :, :T],
                           qs.h.ap().rearrange("(h d) t -> d h t", d=128)[:, 4 * g:4 * g + 4, co:co + T], [qs], [qt])
                    nkb = CTX // 128 if is_ctx else NKB
                    for hh in range(4):
                        h = 4 * g + hh
                        ops_ = O_PS[cnt[1] % 2]
                        lps_ = L_PS[cnt[1] % 2]
                        ot = OT[cnt[1] % 2]
                        cnt[1] += 1
                        LA = 2
                        pend = []

                        def qk(kbk):
                            sp_ = S_PS[cnt[0] % 4]
                            pt_ = PT[cnt[0] % 4]
                            cnt[0] += 1
                            mm(sp_[:, :T], Kt[:, kbk * 128:(kbk + 1) * 128], qt[:, hh, :T], True, True, [Kt, qt], [sp_])
                            act(pt_[:, :T], sp_[:, :T], AF.Exp, [sp_], [pt_], scale=ATTN_SCALE)
                            return pt_

                        def pv(kbk, pt_):
                            mm(ops_[:, :T], Vt[:, kbk, :], pt_[:, :T], kbk == 0, kbk == nkb - 1, [Vt, pt_], [ops_], inc=False)
                            mm(lps_[:, :T], ones_bf[:], pt_[:, :T], kbk == 0, kbk == nkb - 1, [ones_bf, pt_], [lps_], inc=True)

                        for kbk in range(nkb + LA):
                            if kbk < nkb:
                                pend.append((kbk, qk(kbk)))
                            if kbk >= LA:
                                pv(*pend.pop(0))
                        rl = tf()
                        recip(rl[:, :T], lps_[:, :T], [lps_], [rl])
                        tt(ot[:, :T], ops_[:, :T], rl[:, :T], ALU.mult, [ops_, rl], [ot])
                        kb.dma("sp", attn_s.h.ap()[h * 128:(h + 1) * 128, co:co + T], ot[:, :T], [ot], [attn_s])
            kb.barrier()

        if STOP_STAGE <= 4:
            return
        with contextlib.ExitStack() as ph:
            xt = kb.sb(ph, "c_xt", [128, KC, TT], F32)
            hT = kb.sb(ph, "c_hT", [128, KC, TT], BF16)
            at = kb.sb(ph, "c_at", [128, KC, TT], BF16)
            acc = kb.sb(ph, "c_acc", [128, KC, TT], F32)
            Y16 = kb.sb(ph, "c_y16", [128, KC, TT], F32)
            BIG = [kb.sb(ph, f"c_big{i}", [128, KC, TT], BF16) for i in range(4)]
            glh = kb.sb(ph, "c_glh", [128, KC, TT + 2 * HALO], BF16)
            gvraw = [kb.sb(ph, f"c_gvraw{i}", [128, D], F32) for i in range(2)]
            gstat = kb.sb(ph, "c_gstat", [128, 8], F32)
            uT, gvT, mT, zT = BIG

            ctiles = [tl for tl in tiles if not (tl[0] and last)]
            wsC = WStream()
            for (is_ctx, t0, T, co) in ctiles:
                wsC.add(w_attn_o, wsrc(w_attn_o, l, 0, D, 0, D), KC, 1024)
                wsC.add(w_in, wsrc(w_in, l, 0, D, GATE_OFF, GATE_OFF + 1024), KC, 1024)
                wsC.add(w_in, wsrc(w_in, l, 0, D, GU_OFF, GU_OFF + 1024), KC, 1024)
                wsC.add(w_in, wsrc(w_in, l, 0, D, GV_OFF, GV_OFF + 1024), KC, 1024)
                wsC.add(w_gmlp_o, wsrc(w_gmlp_o, l, 0, D, 0, D), KC, 1024)
                wsC.add(w_in, wsrc(w_in, l, 0, D, GATE_OFF + 1024, GATE_OFF + 2048), KC, 1024)
                wsC.add(w_conv_o, wsrc(w_conv_o, l, 0, D, 0, D), KC, 1024)
                wsC.add(w_in, wsrc(w_in, l, 0, D, GATE_OFF + 2048, GATE_OFF + 3072), KC, 1024)
                wsC.add(w_out, wsrc(w_out, l, 0, D, 0, D), KC, 1024)
                for q4 in range(4):
                    wsC.add(w_ff1, wsrc(w_ff1, l, 0, D, q4 * 1024, (q4 + 1) * 1024), KC, 1024)
                for ch in range(2):
                    for kh in range(2):
                        wsC.add(w_ff2, wsrc(w_ff2, l, kh * 2048, (kh + 1) * 2048, ch * 512, (ch + 1) * 512), 16, 512)
            wi = 0

            def gated_acc(pbr, wbg, wvg, j, T, first):
                pg = ps_alloc()
                proj(pg, wbg, wvg, j * 128, hT, T)
                sg = tf()
                act(sg[:, :T], pg[:, :T], AF.Sigmoid, [pg], [sg])
                ps_release(pg)
                if first:
                    tt(acc[:, j, :T], pbr[:, :T], sg[:, :T], ALU.mult, [pbr, sg], [acc])
                else:
                    tmp = tf()
                    tt(tmp[:, :T], pbr[:, :T], sg[:, :T], ALU.mult, [pbr, sg], [tmp])
                    tt(acc[:, j, :T], acc[:, j, :T], tmp[:, :T], ALU.add, [acc, tmp], [acc], eng=POOL)

            for (is_ctx, t0, T, co) in ctiles:
                w = 1 if is_ctx else 0
                src = xc_src if is_ctx else x_src
                NS = T // 128
                kb.dma("sp", xt[:, :, :T], fm(src, t0, t0 + T), [src], [xt])
                kb.dma("sp", hT[:, :, :T], fm(hs, co, co + T), [hs], [hT])
                kb.dma("sp", at[:, :, :T], fm(attn_s, co, co + T), [attn_s], [at])
                gsrc = glc if is_ctx else gls
                c_lo = 0 if ((not is_ctx) and t0 > 0) else HALO
                c_hi = T + 2 * HALO if ((not is_ctx) and t0 + T < TOK) else T + HALO
                kb.dma("sp", glh[:, :, c_lo:c_hi], fm(gsrc, t0 + c_lo, t0 + c_hi), [gsrc], [glh])
                if is_ctx:
                    memset(POOL, glh[:, :, 0:HALO], 0.0, [glh])
                    memset(POOL, glh[:, :, T + HALO:T + 2 * HALO], 0.0, [glh])
                else:
                    if t0 == 0:
                        copy(POOL, glh[:, :, 0:HALO], halb[:, 0, :, :], [halb], [glh])
                    if t0 + T == TOK:
                        copy(POOL, glh[:, :, T + HALO:T + 2 * HALO], halb[:, 1, :, :], [halb], [glh])

                wbo, wvo = wsC.get(wi); wi += 1
                wbg, wvg = wsC.get(wi); wi += 1
                for j in range(KC):
                    pb = ps_alloc()
                    proj(pb, wbo, wvo, j * 128, at, T)
                    gated_acc(pb, wbg, wvg, j, T, True)
                    ps_release(pb)

                if debug and l == 0 and (not is_ctx) and t0 == 0:
                    kb.dma("sp", dbgc.h.ap()[0 * D:1 * D, :].rearrange("(k p) c -> p k c", p=128), acc[:, :, :T], [acc], [dbgc])
                wbu, wvu = wsC.get(wi); wi += 1
                for j in range(KC):
                    pu = ps_alloc()
                    proj(pu, wbu, wvu, j * 128, hT, T)
                    act(uT[:, j, :T], pu[:, :T], AF.Gelu, [pu], [uT])
                    ps_release(pu)
                wbv, wvv = wsC.get(wi); wi += 1
                for s in range(NS):
                    gr = gvraw[s % 2]
                    for half in range(2):
                        pv_ = ps_alloc()
                        for kc in range(KC):
                            mm(pv_[:, :512], hT[:, kc, s * 128:(s + 1) * 128], wvv[:, kc, half * 512:(half + 1) * 512],
                               kc == 0, kc == KC - 1, [hT, wbv], [pv_])
                        act(gr[:, half * 512:(half + 1) * 512], pv_[:, :512], AF.Gelu, [pv_], [gr])
                        ps_release(pv_)
                    junk = tf()
                    for half in range(2):
                        act(junk[:, :512], gr[:, half * 512:(half + 1) * 512], AF.Square, [gr], [junk, gstat],
                            accum_out=gstat[:, half:half + 1])
                    tt(gstat[:, 2:3], gstat[:, 0:1], gstat[:, 1:2], ALU.add, [gstat], [gstat])
                    act(gstat[:, 3:4], gstat[:, 2:3], AF.Sqrt, [gstat], [gstat], bias=EPS, scale=1.0 / D)
                    recip(gstat[:, 4:5], gstat[:, 3:4], [gstat], [gstat])
                    stt(gvT[:].rearrange("p k t -> p (k t)")[:, s * D:(s + 1) * D],
                        gr[:], gstat[:, 4:5], gng[:], ALU.mult, ALU.mult, [gr, gstat, gng], [gvT])
                gv_flat = gvT[:].rearrange("p k t -> p (k t)")
                wbgo, wvgo = wsC.get(wi); wi += 1
                wbg, wvg = wsC.get(wi); wi += 1
                for j in range(KC):
                    gg = j // 2
                    psv = ps_alloc()
                    for s in range(NS):
                        mm(psv[:, s * 128:(s + 1) * 128], gv_flat[:, s * D + j * 128: s * D + (j + 1) * 128], wsT[:, gg, :],
                           True, False, [gvT, wsT], [psv], inc=False)
                        mm(psv[:, s * 128:(s + 1) * 128], onesrow[0:1, :], bsrow[0:1, gg * 128:(gg + 1) * 128],
                           False, True, [onesrow, bsrow], [psv], inc=True)
                    tt(mT[:, j, :T], psv[:, :T], uT[:, j, :T], ALU.mult, [psv, uT], [mT])
                    ps_release(psv)
                for j in range(KC):
                    pb = ps_alloc()
                    proj(pb, wbgo, wvgo, j * 128, mT, T)
                    gated_acc(pb, wbg, wvg, j, T, False)
                    ps_release(pb)

                if debug and l == 0 and (not is_ctx) and t0 == 0:
                    kb.dma("sp", dbgc.h.ap()[1 * D:2 * D, :].rearrange("(k p) c -> p k c", p=128), acc[:, :, :T], [acc], [dbgc])
                sum1 = ps_alloc()
                sum2 = ps_alloc()
                for j in range(KC):
                    ts(Y16[:, j, :T], glh[:, j, 1:1 + T], convw[:, j, 0:1], vecs[:, 2, j:j + 1], ALU.mult, ALU.add,
                       [glh, convw, vecs], [Y16])
                    for k in range(1, CK):
                        stt(Y16[:, j, :T], glh[:, j, k + 1:k + 1 + T], convw[:, j, k:k + 1], Y16[:, j, :T], ALU.mult, ALU.add,
                            [glh, convw, Y16], [Y16])
                    yb = tb()
                    act(yb[:, :T], Y16[:, j, :T], AF.Copy, [Y16], [yb])
                    ysq = tb()
                    act(ysq[:, :T], Y16[:, j, :T], AF.Square, [Y16], [ysq])
                    mm(sum1[:, :T], ones_bf[:], yb[:, :T], j == 0, j == KC - 1, [ones_bf, yb], [sum1], inc=True)
                    mm(sum2[:, :T], ones_bf[:], ysq[:, :T], j == 0, j == KC - 1, [ones_bf, ysq], [sum2], inc=True)
                if debug and l == 0 and (not is_ctx) and t0 == 0:
                    kb.dma("sp", dbgc.h.ap()[4 * D:5 * D, :].rearrange("(k p) c -> p k c", p=128), Y16[:, :, :T], [Y16], [dbgc])
                mu = gvraw[0]
                ts(mu[:, :T], sum1[:, :T], 1.0 / D, None, ALU.mult, None, [sum1], [mu])
                ps_release(sum1)
                msq = tf()
                tt(msq[:, :T], mu[:, :T], mu[:, :T], ALU.mult, [mu], [msq])
                var = tf()
                stt(var[:, :T], sum2[:, :T], 1.0 / D, msq[:, :T], ALU.mult, ALU.subtract, [sum2, msq], [var])
                ps_release(sum2)
                stdc = tf()
                act(stdc[:, :T], var[:, :T], AF.Sqrt, [var], [stdc], bias=EPS, scale=1.0)
                rstdc = gvraw[1]
                recip(rstdc[:, :T], stdc[:, :T], [stdc], [rstdc])
                for j in range(KC):
                    t1 = tf()
                    tt(t1[:, :T], Y16[:, j, :T], mu[:, :T], ALU.subtract, [Y16, mu], [t1])
                    t2 = tf()
                    tt(t2[:, :T], t1[:, :T], rstdc[:, :T], ALU.mult, [t1, rstdc], [t2])
                    act(zT[:, j, :T], t2[:, :T], AF.Silu, [t2, vecs], [zT],
                        bias=vecs[:, 4, j:j + 1], scale=vecs[:, 3, j:j + 1])
                wbco, wvco = wsC.get(wi); wi += 1
                wbg, wvg = wsC.get(wi); wi += 1
                for j in range(KC):
                    pb = ps_alloc()
                    proj(pb, wbco, wvco, j * 128, zT, T)
                    gated_acc(pb, wbg, wvg, j, T, False)
                    ps_release(pb)

                if debug and l == 0 and (not is_ctx) and t0 == 0:
                    kb.dma("sp", dbgc.h.ap()[2 * D:3 * D, :].rearrange("(k p) c -> p k c", p=128), acc[:, :, :T], [acc], [dbgc])
                mb = at
                for j in range(KC):
                    copy(ACT, mb[:, j, :T], acc[:, j, :T], [acc], [mb])
                wbw, wvw = wsC.get(wi); wi += 1
                for j in range(KC):
                    pb = ps_alloc()
                    proj(pb, wbw, wvw, j * 128, mb, T)
                    copy(ACT, Y16[:, j, :T], pb[:, :T], [pb], [Y16])
                    ps_release(pb)
                postnorm_res(Y16, T, der[:, w, 2, :], xt)

                if debug and l == 0 and (not is_ctx) and t0 == 0:
                    kb.dma("sp", dbgc.h.ap()[3 * D:4 * D, :].rearrange("(k p) c -> p k c", p=128), xt[:, :, :T], [xt], [dbgc])
                h2 = hT
                normmod(xt, T, der[:, w, 3, :], der[:, w, 4, :], h2)
                for q4 in range(4):
                    wb1, wv1 = wsC.get(wi); wi += 1
                    hid = BIG[q4]
                    for jj in range(KC):
                        pb = ps_alloc()
                        proj(pb, wb1, wv1, jj * 128, h2, T)
                        r = tf()
                        act(r[:, :T], pb[:, :T], AF.Relu, [pb], [r])
                        ps_release(pb)
                        tt(hid[:, jj, :T], r[:, :T], r[:, :T], ALU.mult, [r], [hid])
                for ch in range(2):
                    wb2a, wv2a = wsC.get(wi); wi += 1
                    wb2b, wv2b = wsC.get(wi); wi += 1
                    for jl in range(4):
                        j = ch * 4 + jl
                        pb = ps_alloc()
                        for kc in range(32):
                            wb2, wv2 = (wb2a, wv2a) if kc < 16 else (wb2b, wv2b)
                            mm(pb[:, :T], wv2[:, kc % 16, jl * 128:(jl + 1) * 128], BIG[kc // 8][:, kc % 8, :T],
                               kc == 0, kc == 31, [wb2, BIG[kc // 8]], [pb])
                        copy(ACT, Y16[:, j, :T], pb[:, :T], [pb], [Y16])
                        ps_release(pb)
                postnorm_res(Y16, T, der[:, w, 5, :], xt)
                dst = xcres if is_ctx else x_dst
                kb.dma("sp", fm(dst, t0, t0 + T), xt[:, :, :T], [xt], [dst])
            kb.barrier()
    kb.barrier()


def _rope_tables():
    rows = SEQ // GRID_W
    row = np.repeat(np.arange(rows), GRID_W).astype(np.float32)
    col = np.tile(np.arange(GRID_W), rows).astype(np.float32)
    nf = HD // 4
    inv = (np.float32(10000.0) ** (-(np.arange(nf, dtype=np.float32) / np.float32(nf)))).astype(np.float32)
    ang = np.concatenate([row[:, None] * inv, col[:, None] * inv], axis=-1).astype(np.float32)
    cos = np.cos(ang).astype(np.float32)
    sin = np.sin(ang).astype(np.float32)
    cosT = np.concatenate([cos.T, cos.T], axis=0)
    sinT = np.concatenate([-sin.T, sin.T], axis=0)
    return np.ascontiguousarray(cosT), np.ascontiguousarray(sinT)


def _pp(v):
    return np.ascontiguousarray(np.asarray(v, np.float32).reshape(KC, 128).T)


def make_in_maps(x, c, ctx, c_ctx, ada_w, ada_b, mix_pre_g, mix_post_g, w_in, q_norm_g, k_norm_g,
                 w_attn_o, gmlp_norm_g, gmlp_ws, gmlp_bs, w_gmlp_o, conv_w, conv_b, conv_norm_g,
                 conv_norm_b, w_conv_o, w_out, ffn_pre_g, ffn_post_g, w_ff1, w_ff2):
    f = lambda a: np.ascontiguousarray(np.asarray(a, dtype=np.float32))
    x = f(x); ctx = f(ctx)
    xT = np.ascontiguousarray(x[0].T)
    ctxT = np.ascontiguousarray(ctx[0].T)
    cT = np.stack([_pp(f(c)[0]), _pp(f(c_ctx))], axis=-1).reshape(128, KC * 2)
    L = DEPTH
    vecs = np.zeros((L, 128, 7, KC), np.float32)
    for l in range(L):
        for i, v in enumerate([mix_pre_g, mix_post_g, conv_b, conv_norm_g, conv_norm_b, ffn_pre_g, ffn_post_g]):
            vecs[l, :, i, :] = _pp(f(v)[l])
    vecs = vecs.reshape(L, 128, 7 * KC)
    adab = np.stack([f(ada_b)[l].reshape(48, 128).T for l in range(L)])
    qkg = np.stack([np.stack([f(q_norm_g)[l], f(k_norm_g)[l]], axis=-1) for l in range(L)])
    gng = np.stack([np.broadcast_to(f(gmlp_norm_g)[l][None, :], (128, D)) for l in range(L)])
    convw = np.stack([f(conv_w)[l].T.reshape(KC, 128, CK).transpose(1, 0, 2).reshape(128, KC * CK) for l in range(L)])
    wsT = np.stack([f(gmlp_ws)[l].transpose(2, 0, 1).reshape(128, 4 * 128) for l in range(L)])
    bsrow = np.stack([f(gmlp_bs)[l].reshape(1, 512) for l in range(L)])
    cosT, sinT = _rope_tables()
    ident = np.eye(128, dtype=np.float32)
    perm = np.zeros((128, 128), np.float32)
    for m in range(128):
        perm[(m + 64) % 128, m] = 1.0
    shared = dict(ctxT=ctxT, cT=np.ascontiguousarray(cT), ada_w=f(ada_w), w_in=f(w_in), w_attn_o=f(w_attn_o),
                  w_gmlp_o=f(w_gmlp_o), w_conv_o=f(w_conv_o), w_out=f(w_out), w_ff1=f(w_ff1), w_ff2=f(w_ff2),
                  vecs=np.ascontiguousarray(vecs), adab=np.ascontiguousarray(adab), qkg=np.ascontiguousarray(qkg),
                  gng=np.ascontiguousarray(gng), convw=np.ascontiguousarray(convw), wsT=np.ascontiguousarray(wsT),
                  bsrow=np.ascontiguousarray(bsrow), ident=ident, perm=perm)
    maps = []
    for r in range(NCORES):
        m = dict(shared)
        m["xT"] = np.ascontiguousarray(xT[:, r * TOK:(r + 1) * TOK])
        m["cosT"] = np.ascontiguousarray(cosT[:, r * TOK:(r + 1) * TOK])
        m["sinT"] = np.ascontiguousarray(sinT[:, r * TOK:(r + 1) * TOK])
        sel = np.zeros((128, 16), np.float32)
        if r > 0:
            sel[:, r - 1] = 1.0
        if r < NCORES - 1:
            sel[:, 8 + r + 1] = 1.0
        m["sel"] = sel
        maps.append(m)
    return maps


_NC_CACHE = {}


def kernel(**inputs):
    maps = make_in_maps(**inputs)
    if "nc" not in _NC_CACHE:
        _NC_CACHE["nc"] = build_program()
    nc = _NC_CACHE["nc"]
    res = run_bass_kernel_spmd(nc, maps, core_ids=list(range(NCORES)))
    outT = np.concatenate([np.asarray(res.results[r]["outT"]) for r in range(NCORES)], axis=1)
    return np.ascontiguousarray(outT.T)[None, :, :].astype(np.float32)
```
